# Optimizing a Trainium2 kernel written in Bass

```python
import math
import jax, jax.numpy as jnp
from jax import lax
import numpy as np

D_MODEL = 1024
BATCH = 4
SEQ = 8192
DEPTH = 1

A_HEADS = 4
A_HEAD_DIM = D_MODEL // A_HEADS
A_WIDTH = A_HEADS * A_HEAD_DIM
A_CHUNK = 128
CONV_WIDTH = 4
F_BIAS_LO = 3.0
F_BIAS_HI = 6.0
B_HEAD_DIM = 128
B_HEADS = D_MODEL // B_HEAD_DIM
B_WIDTH = B_HEADS * B_HEAD_DIM
B_BLOCK = 128
NORM_EPS = 1e-6

COLUMN_SIZES = (
    A_WIDTH, A_WIDTH, A_WIDTH,
    A_WIDTH,
    A_WIDTH,
    A_HEADS, A_HEADS,
    B_WIDTH, B_WIDTH, B_WIDTH,
    B_WIDTH,
    D_MODEL, D_MODEL,
)
IN_WIDTH = sum(COLUMN_SIZES)

kernel_name = "hybrid_mlstm_stickbreaking_gated"


def rms_norm(x, gain):
    xf = x.astype(jnp.float32)
    y = xf * lax.rsqrt(jnp.mean(xf * xf, axis=-1, keepdims=True) + NORM_EPS)
    return (y * gain.astype(jnp.float32)).astype(x.dtype)


def split_columns(proj):
    points = [int(p) for p in np.cumsum(np.array(COLUMN_SIZES))[:-1]]
    return jnp.split(proj, points, axis=-1)


def to_heads(t, n_heads):
    b, s, w = t.shape
    return t.reshape(b, s, n_heads, w // n_heads).transpose(0, 2, 1, 3)


def causal_depthwise_conv(x, w, bias):
    k_width, ch = w.shape
    y = lax.conv_general_dilated(
        x, w[:, None, :].astype(x.dtype), window_strides=(1,), padding=[(k_width - 1, 0)],
        dimension_numbers=("NWC", "WIO", "NWC"), feature_group_count=ch)
    return y + bias.astype(x.dtype)


def mlstm_chunkwise(q, k, v, i_pre, f_pre):
    b, h, s, d = q.shape
    L = A_CHUNK
    nc = s // L
    f32 = jnp.float32
    qc = q.astype(f32).reshape(b, h, nc, L, d)
    kc = (k.astype(f32) / math.sqrt(d)).reshape(b, h, nc, L, d)
    vc = v.astype(f32).reshape(b, h, nc, L, d)
    ic = i_pre.reshape(b, h, nc, L)
    logf = jax.nn.log_sigmoid(f_pre).reshape(b, h, nc, L)
    bcum = jnp.cumsum(logf, axis=-1)
    b_last = bcum[..., -1]

    a = b_last[..., None] - bcum + ic
    a_max = jnp.max(a, axis=-1)
    wa = jnp.exp(a - a_max[..., None])
    c_chunk = jnp.einsum("bhcsv,bhcsk->bhcvk", vc * wa[..., None], kc)
    n_chunk = jnp.einsum("bhcs,bhcsk->bhck", wa, kc)

    def step(carry, inp):
        c_state, n_state, m_state = carry
        cc, ncn, amx, bl = inp
        m_new = jnp.maximum(bl + m_state, amx)
        decay = jnp.exp(bl + m_state - m_new)
        scale = jnp.exp(amx - m_new)
        c_new = decay[..., None, None] * c_state + scale[..., None, None] * cc
        n_new = decay[..., None] * n_state + scale[..., None] * ncn
        return (c_new, n_new, m_new), (c_state, n_state, m_state)

    init = (jnp.zeros((b, h, d, d), f32), jnp.zeros((b, h, d), f32), jnp.zeros((b, h), f32))
    xs = (jnp.moveaxis(c_chunk, 2, 0), jnp.moveaxis(n_chunk, 2, 0),
          jnp.moveaxis(a_max, 2, 0), jnp.moveaxis(b_last, 2, 0))
    _, (c_prev, n_prev, m_prev) = lax.scan(step, init, xs)
    c_prev = jnp.moveaxis(c_prev, 0, 2)
    n_prev = jnp.moveaxis(n_prev, 0, 2)
    m_prev = jnp.moveaxis(m_prev, 0, 2)

    causal = jnp.tril(jnp.ones((L, L), dtype=bool))
    log_d = bcum[..., :, None] - bcum[..., None, :] + ic[..., None, :]
    log_d = jnp.where(causal, log_d, -jnp.inf)
    log_inter = bcum + m_prev[..., None]
    m_t = jnp.maximum(log_inter, jnp.max(log_d, axis=-1))
    dmat = jnp.exp(log_d - m_t[..., None])
    inter_w = jnp.exp(log_inter - m_t)
    scores = jnp.einsum("bhctd,bhcsd->bhcts", qc, kc) * dmat
    num = (jnp.einsum("bhcts,bhcsd->bhctd", scores, vc)
           + inter_w[..., None] * jnp.einsum("bhcvk,bhctk->bhctv", c_prev, qc))
    den = jnp.sum(scores, axis=-1) + inter_w * jnp.einsum("bhck,bhctk->bhct", n_prev, qc)
    h_tilde = num / jnp.maximum(jnp.abs(den), jnp.exp(-m_t))[..., None]
    return h_tilde.reshape(b, h, s, d)


def stick_breaking_attention(q, k, v):
    b, h, s, d = q.shape
    scale = 1.0 / math.sqrt(d)
    outs = []
    for blk in range(s // B_BLOCK):
        start = blk * B_BLOCK
        end = start + B_BLOCK
        qb = q[:, :, start:end].astype(jnp.float32)
        kp = k[:, :, :end].astype(jnp.float32)
        vp = v[:, :, :end].astype(jnp.float32)
        z = jnp.einsum("bhqd,bhkd->bhqk", qb, kp) * scale
        mask = jnp.arange(end)[None, :] < jnp.arange(start, end)[:, None]
        log_keep = jnp.where(mask, jax.nn.log_sigmoid(-z), 0.0)
        later = lax.cumsum(log_keep, axis=3, reverse=True) - log_keep
        weights = jnp.where(mask, jnp.exp(jax.nn.log_sigmoid(z) + later), 0.0)
        outs.append(jnp.einsum("bhqk,bhkd->bhqd", weights, vp))
    return jnp.concatenate(outs, axis=2).astype(q.dtype)


def hybrid_layer(x, norm_gain, w_in, conv_w, conv_b, i_bias, f_bias, a_head_gain,
                 qk_gain_q, qk_gain_k, w_branch_a, w_branch_b, w_out):
    bsz, s, _ = x.shape
    hn = rms_norm(x, norm_gain)
    proj = jnp.einsum("bsd,dn->bsn", hn, w_in)
    (qa, ka, va, oa, za, ia, fa, qb, kb, vb, zb, ga, gb) = split_columns(proj)

    qk_a = jax.nn.silu(causal_depthwise_conv(jnp.concatenate([qa, ka], axis=-1), conv_w, conv_b))
    qa_c, ka_c = jnp.split(qk_a, 2, axis=-1)
    i_pre = (ia + i_bias).astype(jnp.float32).transpose(0, 2, 1)
    f_pre = (fa + f_bias).astype(jnp.float32).transpose(0, 2, 1)
    h_tilde = mlstm_chunkwise(to_heads(qa_c, A_HEADS), to_heads(ka_c, A_HEADS),
                              to_heads(va, A_HEADS), i_pre, f_pre)
    h_tilde = h_tilde.transpose(0, 2, 1, 3).astype(x.dtype)
    o_gate = jax.nn.sigmoid(oa).reshape(bsz, s, A_HEADS, A_HEAD_DIM)
    h_a = rms_norm(o_gate * h_tilde, a_head_gain).reshape(bsz, s, A_WIDTH)
    y_a = jnp.einsum("bsw,wd->bsd", h_a * jax.nn.silu(za), w_branch_a)

    q_b = rms_norm(to_heads(qb, B_HEADS), qk_gain_q)
    k_b = rms_norm(to_heads(kb, B_HEADS), qk_gain_k)
    o_b = stick_breaking_attention(q_b, k_b, to_heads(vb, B_HEADS))
    o_b = o_b.transpose(0, 2, 1, 3).reshape(bsz, s, B_WIDTH)
    y_b = jnp.einsum("bsw,wd->bsd", o_b * jax.nn.silu(zb), w_branch_b)

    merged = jax.nn.sigmoid(ga) * y_a + jax.nn.sigmoid(gb) * y_b
    return x + jnp.einsum("bsd,de->bse", merged, w_out)


def setup_inputs(seed: int = 0) -> dict:
    key = jax.random.key(seed)
    ks = jax.random.split(key, 14)
    f32 = jnp.float32
    x = jax.random.normal(ks[0], (BATCH, SEQ, D_MODEL), f32)
    norm_gain = 1.0 + 0.02 * jax.random.normal(ks[1], (DEPTH, D_MODEL), f32)
    w_in = jax.random.normal(ks[2], (DEPTH, D_MODEL, IN_WIDTH), f32) * D_MODEL ** -0.5
    conv_w = jax.random.normal(ks[3], (DEPTH, CONV_WIDTH, 2 * A_WIDTH), f32) * CONV_WIDTH ** -0.5
    conv_b = 0.01 * jax.random.normal(ks[4], (DEPTH, 2 * A_WIDTH), f32)
    i_bias = 0.1 * jax.random.normal(ks[5], (DEPTH, A_HEADS), f32)
    f_bias = (jnp.linspace(F_BIAS_LO, F_BIAS_HI, A_HEADS, dtype=f32)[None, :]
              + 0.1 * jax.random.normal(ks[6], (DEPTH, A_HEADS), f32))
    a_head_gain = 1.0 + 0.02 * jax.random.normal(ks[7], (DEPTH, A_HEADS, A_HEAD_DIM), f32)
    qk_gain_q = 1.0 + 0.02 * jax.random.normal(ks[8], (DEPTH, B_HEAD_DIM), f32)
    qk_gain_k = 1.0 + 0.02 * jax.random.normal(ks[9], (DEPTH, B_HEAD_DIM), f32)
    w_branch_a = jax.random.normal(ks[10], (DEPTH, A_WIDTH, D_MODEL), f32) * A_WIDTH ** -0.5
    w_branch_b = jax.random.normal(ks[11], (DEPTH, B_WIDTH, D_MODEL), f32) * B_WIDTH ** -0.5
    w_out = jax.random.normal(ks[12], (DEPTH, D_MODEL, D_MODEL), f32) * D_MODEL ** -0.5
    return {"x": x, "norm_gain": norm_gain, "w_in": w_in, "conv_w": conv_w, "conv_b": conv_b,
            "i_bias": i_bias, "f_bias": f_bias, "a_head_gain": a_head_gain,
            "qk_gain_q": qk_gain_q, "qk_gain_k": qk_gain_k, "w_branch_a": w_branch_a,
            "w_branch_b": w_branch_b, "w_out": w_out}


def reference(x, norm_gain, w_in, conv_w, conv_b, i_bias, f_bias, a_head_gain,
              qk_gain_q, qk_gain_k, w_branch_a, w_branch_b, w_out):
    for layer in range(DEPTH):
        x = hybrid_layer(x, norm_gain[layer], w_in[layer], conv_w[layer], conv_b[layer],
                         i_bias[layer], f_bias[layer], a_head_gain[layer], qk_gain_q[layer],
                         qk_gain_k[layer], w_branch_a[layer], w_branch_b[layer], w_out[layer])
    return x
```

```python
import contextlib
import numpy as np
import ml_dtypes
import concourse.bass as bass
import concourse.mybir as mybir
from concourse.bass_utils import run_bass_kernel_spmd

F32 = mybir.dt.float32
BF16 = mybir.dt.bfloat16
U8 = mybir.dt.uint8
AF = mybir.ActivationFunctionType
ALU = mybir.AluOpType

S = 8192
D = 1024
NBLK = 64
OWN = 4096
HALO = 128
TOKW = OWN + HALO
INW = 11272
C_QA, C_KA, C_VA, C_OA, C_ZA, C_IG, C_QB, C_KB, C_VB, C_ZB, C_GA, C_GB = (
    0, 1024, 2048, 3072, 4096, 5120, 5128, 6152, 7176, 8200, 9224, 10248)
EPS = 1e-6

P_GN = 0
P_CW = 8
P_CB = 72
P_IB = 88
P_FB = 92
P_AHG = 96
P_GQ = 104
P_GK = 105
NPAR = 106
CF_ONES, CF_TRI, CF_ID = 0, 128, 256
NCF = 384
CB_ONES, CB_ID, CB_TRI, CB_REST = 0, 128, 256, 384
NCB = 512


class Tl:
    __slots__ = ("name", "w", "r", "sem", "dcount")

    def __init__(self, name):
        self.name = name
        self.w = {}
        self.r = {}
        self.sem = None
        self.dcount = 0


class Buf:
    __slots__ = ("ap", "t")

    def __init__(self, ap, name):
        self.ap = ap
        self.t = Tl(name)

    def __getitem__(self, k):
        return self.ap[k]


class Op:
    __slots__ = ("eng", "fn", "idx", "deps", "sig", "dma_t", "dval", "sigcount", "raw")

    def __init__(self, eng, fn):
        self.eng = eng
        self.fn = fn
        self.deps = None
        self.sig = False
        self.dma_t = None
        self.dval = 0
        self.sigcount = 0
        self.raw = ()


ENGS = ("pe", "act", "dve", "pool", "sp")


class Builder:
    def __init__(self, nc, same_engine_sync=False):
        self.nc = nc
        self.prog = {e: [] for e in ENGS}
        self.same_engine_sync = same_engine_sync
        self.dma_tiles = []
        self.arena_off = 0
        self.arena = None
        self.arena_size = 0

    def init_arena(self, nbytes):
        self.arena = self.nc.alloc_sbuf_tensor("arena", [128, nbytes], U8)
        self.arena_size = nbytes

    def alloc(self, name, shape, dt):
        esz = 4 if dt == F32 else 2
        n = int(np.prod(shape[1:]))
        nb = n * esz
        off = (self.arena_off + 63) // 64 * 64
        assert off + nb <= self.arena_size, f"arena overflow at {name}: {off + nb}"
        self.arena_off = off + nb
        ap = self.arena[:, off:off + nb].bitcast(dt)
        if len(shape) == 3:
            ap = ap.rearrange("p (a b) -> p a b", b=shape[2])
        elif len(shape) == 4:
            ap = ap.rearrange("p (a b c) -> p a b c", b=shape[2], c=shape[3])
        if shape[0] != 128:
            ap = ap[0:shape[0]]
        return Buf(ap, name)

    def mark(self):
        return self.arena_off

    def release(self, m):
        self.arena_off = m

    def _record(self, o, r, w, pw):
        deps = {}
        raw = set()

        def add(d):
            if d is not o:
                deps[id(d)] = d
        for b in r:
            for d in b.t.w.values():
                add(d)
                raw.add(id(d))
        for b in w:
            for d in b.t.w.values():
                add(d)
            for d in b.t.r.values():
                add(d)
        for b in pw:
            for d in b.t.w.values():
                add(d)
            for d in b.t.r.values():
                add(d)
        o.deps = list(deps.values())
        o.raw = raw
        key = o.eng if o.dma_t is None else ("dma", id(o.dma_t))
        for b in r:
            b.t.r[key] = o
        for b in w:
            b.t.w = {key: o}
            b.t.r = {}
        for b in pw:
            b.t.w[key] = o
        o.idx = len(self.prog[o.eng])
        self.prog[o.eng].append(o)
        return o

    def cop(self, eng, fn, r=(), w=(), pw=()):
        return self._record(Op(eng, fn), r, w, pw)

    def dma(self, q, out, in_, sb, r=(), w=(), pw=()):
        o = Op(q, lambda e: e.dma_start(out=out, in_=in_))
        t = sb.t
        if t.sem is None:
            t.sem = True
            self.dma_tiles.append(t)
        t.dcount += 1
        o.dma_t = t
        o.dval = 16 * t.dcount
        return self._record(o, r, w, pw)

    def barrier(self):
        lasts = [self.prog[e][-1] for e in ENGS if self.prog[e]]
        dl = []
        for t in self.dma_tiles:
            if t.dcount:
                d = Op("sp", None)
                d.dma_t = t
                d.dval = 16 * t.dcount
                dl.append(d)
        for e in ENGS:
            o = Op(e, None)
            o.deps = [x for x in lasts if x.eng != e] + dl
            o.idx = len(self.prog[e])
            self.prog[e].append(o)

    def emit(self):
        nc = self.nc
        es = contextlib.ExitStack()
        engsem = {e: es.enter_context(nc.semaphore("s_" + e)) for e in ENGS}
        for i, t in enumerate(self.dma_tiles):
            t.sem = es.enter_context(nc.semaphore(f"d{i}"))
        for e in ENGS:
            for o in self.prog[e]:
                for d in o.deps:
                    if d.dma_t is None:
                        d.sig = True
        for e in ENGS:
            c = 0
            for o in self.prog[e]:
                if o.sig:
                    c += 1
                o.sigcount = c
        engobj = {"pe": nc.tensor, "act": nc.scalar, "dve": nc.vector, "pool": nc.gpsimd, "sp": nc.sync}
        nwaits = 0
        for e in ENGS:
            eo = engobj[e]
            waited = {}
            for o in self.prog[e]:
                need = {}
                for d in o.deps:
                    if d.dma_t is not None:
                        k, v = ("d", id(d.dma_t)), d.dval
                        sem = d.dma_t.sem
                    else:
                        if d.eng == e and (e == "pe" or e == "sp" or
                                           (not self.same_engine_sync and id(d) not in o.raw)):
                            continue
                        k, v = ("e", d.eng), d.sigcount
                        sem = engsem[d.eng]
                    if need.get(k, (0, None))[0] < v:
                        need[k] = (v, sem)
                for k, (v, sem) in need.items():
                    if waited.get(k, 0) >= v:
                        continue
                    eo.wait_ge(sem, v)
                    nwaits += 1
                    waited[k] = v
                if o.fn is None:
                    continue
                ins = o.fn(eo)
                if o.dma_t is not None:
                    ins.then_inc(o.dma_t.sem, 16)
                elif o.sig:
                    ins.then_inc(engsem[e], 1)
        for t in self.dma_tiles:
            if t.dcount:
                nc.gpsimd.wait_ge(t.sem, 16 * t.dcount)
        self.stats = {e: len(self.prog[e]) for e in ENGS}
        self.stats["waits"] = nwaits
        return es


def build_program(debug=False, stop_after=None):
    nc = bass.Bass("TRN2", target_bir_lowering=False)
    B = Builder(nc)
    okind = "ExternalOutput" if debug else "Internal"

    XT = nc.dram_tensor("xt", [3, D, TOKW], F32, kind="ExternalInput").ap()
    WIN = nc.dram_tensor("w_in", [D, INW], F32, kind="ExternalInput").ap()
    WA = nc.dram_tensor("w_a", [D, D], F32, kind="ExternalInput").ap()
    WB = nc.dram_tensor("w_b", [D, D], F32, kind="ExternalInput").ap()
    WO = nc.dram_tensor("w_o", [D, D], F32, kind="ExternalInput").ap()
    PAR = nc.dram_tensor("params", [128, NPAR], F32, kind="ExternalInput").ap()
    CF = nc.dram_tensor("cf", [128, NCF], F32, kind="ExternalInput").ap()
    CBF = nc.dram_tensor("cbf", [128, NCB], BF16, kind="ExternalInput").ap()
    MAD = nc.dram_tensor("ma", [128, 8, 512], BF16, kind="ExternalInput").ap()
    MMD = nc.dram_tensor("mm", [128, 2, 128], F32, kind="ExternalInput").ap()
    OUT = nc.dram_tensor("out", [D, OWN], F32, kind="ExternalOutput").ap()

    def scr(name, shape, dt=BF16):
        return Buf(nc.dram_tensor(name, shape, dt, kind=okind).ap(), name)
    KTA = scr("kta", [8, 128, S])
    KA = scr("ka", [NBLK, 128, 1024])
    VA = scr("va", [NBLK, 128, 1024])
    KTB = scr("ktb", [8, 128, S])
    VB = scr("vb", [NBLK, 128, 1024])
    QTA = scr("qta", [8, 128, OWN])
    OT = scr("ot", [8, 128, OWN])
    ZAT = scr("zat", [8, 128, OWN])
    QTB = scr("qtb", [8, 128, OWN])
    ZBT = scr("zbt", [8, 128, OWN])
    GAT = scr("gat", [8, 128, OWN])
    GBT = scr("gbt", [8, 128, OWN])
    HA = scr("ha", [8, 128, OWN])
    HB = scr("hb", [8, 128, OWN])
    GTD = scr("gtd", [128, NBLK, 8], F32) if debug else None

    B.init_arena(207 * 1024)
    PS = [Buf(nc.alloc_psum_tensor(f"ps{i}", [128, 512], F32)[:], f"ps{i}") for i in range(8)]
    psi = [0]

    def nps():
        p = PS[psi[0] % 8]
        psi[0] += 1
        return p

    par = B.alloc("par", [128, NPAR], F32)
    cf = B.alloc("cf", [128, NCF], F32)
    cb = B.alloc("cb", [128, NCB], BF16)
    GT = B.alloc("gt", [128, NBLK, 8], F32)
    B.dma("sp", par.ap, PAR, par, w=[par])
    B.dma("sp", cf.ap, CF, cf, w=[cf])
    B.dma("sp", cb.ap, CBF, cb, w=[cb])
    B.cop("dve", lambda e: e.tensor_scalar(out=par[:, P_GK:P_GK + 1], in0=par[:, P_GK:P_GK + 1],
                                           scalar1=float(128 ** -0.5), scalar2=None, op0=ALU.mult),
          r=[par], w=[par])
    ones_b = cb[:, CB_ONES:CB_ONES + 128]
    id_b = cb[:, CB_ID:CB_ID + 128]
    tri_b = cb[:, CB_TRI:CB_TRI + 128]
    rest_b = cb[:, CB_REST:CB_REST + 128]
    ones_f = cf[:, CF_ONES:CF_ONES + 128]
    tri_f = cf[:, CF_TRI:CF_TRI + 128]

    base_mark = B.mark()

    hn = B.alloc("hn", [128, 8, TOKW], BF16)
    hn_t = [Buf(None, f"hn{j}") for j in range(9)]
    st32 = [B.alloc(f"st32_{i}", [128, 8, 512], F32) for i in range(2)]
    wbf = [B.alloc(f"wbf{i}", [128, 8, 512], BF16) for i in range(3)]
    t1 = [B.alloc(f"t1_{i}", [128, 512], F32) for i in range(2)]
    rstd = [B.alloc(f"rstd{i}", [128, 512], F32) for i in range(2)]
    sq1 = [B.alloc(f"sq1_{i}", [128, 512], BF16) for i in range(2)]
    ost = [B.alloc(f"ost{i}", [128, OWN], BF16) for i in range(2)]
    osth = [Buf(ost[1].ap[:, 0:2048], "osth0"), Buf(ost[1].ap[:, 2048:4096], "osth1")]
    pext = B.alloc("pext", [128, 32, 132], F32)
    acc = B.alloc("acc", [128, 32, 128], F32)
    sqbs = []
    for i in range(4):
        q_ = Buf(acc.ap.rearrange("p a b -> p (a b)")[:, 1024 * i:1024 * (i + 1)].bitcast(BF16).rearrange(
            "p (a b) -> p a b", b=256), f"sqb{i}")
        sqbs.append(q_)
    st4 = []
    for i in range(4):
        st4.append(Buf(st32[i // 2].ap[:, :, (i % 2) * 256:(i % 2) * 256 + 256], f"st4_{i}"))
    ktm = B.alloc("ktm", [128, 32, 128], BF16)
    vst = [B.alloc(f"vst{i}", [128, 4, 512], BF16) for i in range(2)]
    wg32 = B.alloc("wg32", [128, 8, 8], F32)
    wgb = B.alloc("wgb", [128, 8, 8], BF16)
    gtmp = B.alloc("gtmp", [128, 32, 4], F32)
    cnt = {"st": 0, "w": 0, "wf": 0, "ost": 0, "vst": 0, "r": 0}

    def load_weights(col0, ncols=512, wslot=None):
        if wslot == "c":
            k = 0
        elif wslot == "f":
            k = 1 + cnt["wf"] % 2
            cnt["wf"] += 1
        else:
            k = cnt["w"] % 3
            cnt["w"] += 1
        wb_ = wbf[k]
        B.dma("pool", wb_.ap[:, :, 0:ncols], WIN[:, col0:col0 + ncols].rearrange("(c p) n -> p c n", p=128),
              wb_, w=[wb_])
        return wb_

    def normalize(P):
        info = {}
        tiles = [(c0, 256) for c0 in range(0, OWN, 256)] + [(OWN, HALO)]

        def stage1(j):
            c0, W = tiles[j]
            k = cnt["st"] % 4
            cnt["st"] += 1
            st, sq = st4[k], sqbs[k]
            B.dma("sp", st.ap[:, :, 0:W], XT[P, :, c0:c0 + W].rearrange("(c p) t -> p c t", p=128),
                  st, w=[st])
            B.cop("act", lambda e: e.activation(out=sq.ap[:, :, 0:W], in_=st.ap[:, :, 0:W], func=AF.Square),
                  r=[st], w=[sq] + ([acc] if j < 4 else []))
            ps = nps()
            for c in range(8):
                B.cop("pe", lambda e, c=c: e.matmul(ps.ap[:, 0:W], lhsT=ones_b, rhs=sq.ap[:, c, 0:W],
                                                    start=(c == 0), stop=(c == 7)),
                      r=[sq, cb], w=[ps] if c == 0 else (), pw=[ps] if c else ())
            info[j] = (st, ps, c0, W)

        def stage2(j):
            st, ps, c0, W = info[j]
            k = cnt["r"] % 2
            cnt["r"] += 1
            tt, rs = t1[k], rstd[k]
            ht = hn_t[min(c0 // 512, 8)]
            first = (c0 % 512 == 0)
            B.cop("act", lambda e: e.activation(out=tt.ap[:, 0:W], in_=ps.ap[:, 0:W], func=AF.Ln,
                                                scale=1.0 / D, bias=EPS), r=[ps], w=[tt])
            B.cop("act", lambda e: e.activation(out=rs.ap[:, 0:W], in_=tt.ap[:, 0:W], func=AF.Exp,
                                                scale=-0.5), r=[tt], w=[rs])
            for c in range(8):
                B.cop("dve", lambda e, c=c: e.scalar_tensor_tensor(
                    out=hn.ap[:, c, c0:c0 + W], in0=st.ap[:, c, 0:W], scalar=par[:, P_GN + c:P_GN + c + 1],
                    in1=rs.ap[:, 0:W], op0=ALU.mult, op1=ALU.mult),
                    r=[st, rs, par], w=[ht] if (c == 0 and first) else (), pw=() if (c == 0 and first) else [ht])
        nt = len(tiles)
        stage1(0)
        stage1(1)
        for j in range(nt):
            if j + 2 < nt:
                stage1(j + 2)
            stage2(j)

    def fm_group(P, col0, typ, cc0, dest, gain_col=None, wslot=None):
        wb_ = load_weights(col0, wslot=wslot)
        tok0 = P * OWN if P < 2 else 0
        isconv = typ in ("convq", "convk")
        for nci in range(4):
            cc = cc0 + nci
            o_ = ost[0]
            ntl = 9 if isconv else 8
            for j in range(ntl):
                c0 = j * 512
                W = 512 if j < 8 else HALO
                if not isconv:
                    if j % 4 == 0:
                        o_ = osth[cnt["ost"] % 2]
                        cnt["ost"] += 1
                oc0 = (j % 4) * 512
                ps = nps()
                for c in range(8):
                    B.cop("pe", lambda e, ps=ps, c=c, W=W, c0=c0, nci=nci: e.matmul(
                        ps.ap[:, 0:W], lhsT=wb_.ap[:, c, nci * 128:(nci + 1) * 128], rhs=hn.ap[:, c, c0:c0 + W],
                        start=(c == 0), stop=(c == 7)),
                        r=[wb_, hn_t[j]], w=[ps] if c == 0 else (), pw=[ps] if c else ())
                if typ == "sig" or typ == "silu":
                    fn = AF.Sigmoid if typ == "sig" else AF.Silu
                    B.cop("act", lambda e, ps=ps, c0=oc0, fn=fn, o_=o_: e.activation(out=o_.ap[:, c0:c0 + 512], in_=ps.ap,
                                                                                   func=fn),
                          r=[ps], w=[o_] if j % 4 == 0 else (), pw=[o_] if j % 4 else ())
                elif typ == "qkn":
                    k = cnt["r"] % 2
                    cnt["r"] += 1
                    s1, tt, rs = sq1[k], t1[k], rstd[k]
                    B.cop("act", lambda e, ps=ps, s1=s1: e.activation(out=s1.ap, in_=ps.ap, func=AF.Square),
                          r=[ps], w=[s1])
                    ps2 = nps()
                    B.cop("pe", lambda e, ps2=ps2, s1=s1: e.matmul(ps2.ap, lhsT=ones_b, rhs=s1.ap, start=True, stop=True),
                          r=[s1, cb], w=[ps2])
                    B.cop("act", lambda e, ps2=ps2, tt=tt: e.activation(out=tt.ap, in_=ps2.ap, func=AF.Ln,
                                                                       scale=1.0 / 128, bias=EPS), r=[ps2], w=[tt])
                    B.cop("act", lambda e, tt=tt, rs=rs: e.activation(out=rs.ap, in_=tt.ap, func=AF.Exp, scale=-0.5),
                          r=[tt], w=[rs])
                    B.cop("dve", lambda e, ps=ps, rs=rs, c0=oc0, o_=o_: e.scalar_tensor_tensor(
                        out=o_.ap[:, c0:c0 + 512], in0=ps.ap, scalar=par[:, gain_col:gain_col + 1], in1=rs.ap,
                        op0=ALU.mult, op1=ALU.mult),
                        r=[ps, rs, par], w=[o_] if j % 4 == 0 else (), pw=[o_] if j % 4 else ())
                else:
                    if j < 8:
                        B.cop("act", lambda e, ps=ps, j=j: e.activation(
                            out=pext.ap[:, 4 * j:4 * j + 4, 4:132],
                            in_=ps.ap.rearrange("p (a b) -> p a b", b=128), func=AF.Copy),
                            r=[ps], w=[pext] if j == 0 else (), pw=[pext] if j else ())
                    else:
                        B.cop("act", lambda e, ps=ps: e.activation(
                            out=pext.ap[:, :, 0:4], in_=ps.ap[:, 0:128].rearrange("p (a b) -> p a b", b=4),
                            func=AF.Copy), r=[ps], pw=[pext])
                if (not isconv) and j % 4 == 3:
                    hb0 = tok0 + (j // 4) * 2048
                    B.dma("sp", dest.ap[cc, :, hb0:hb0 + 2048], o_.ap, o_, r=[o_], pw=[dest])
            if typ in ("convq", "convk"):
                pc = P_CW + (cc if typ == "convq" else 8 + cc) * 4
                pb = P_CB + (cc if typ == "convq" else 8 + cc)
                B.cop("dve", lambda e, pc=pc, pb=pb: e.tensor_scalar(
                    out=acc.ap, in0=pext.ap[:, :, 1:129], scalar1=par[:, pc:pc + 1], scalar2=par[:, pb:pb + 1],
                    op0=ALU.mult, op1=ALU.add), r=[pext, par], w=[acc])
                for k in range(1, 4):
                    B.cop("dve", lambda e, pc=pc, k=k: e.scalar_tensor_tensor(
                        out=acc.ap, in0=pext.ap[:, :, k + 1:k + 129], scalar=par[:, pc + k:pc + k + 1], in1=acc.ap,
                        op0=ALU.mult, op1=ALU.add), r=[pext, par, acc], w=[acc])
                yield
                o3 = o_.ap.rearrange("p (a b) -> p a b", b=128)
                if typ == "convk":
                    B.cop("act", lambda e, o3=o3: e.activation(out=o3, in_=acc.ap, func=AF.Silu), r=[acc], w=[o_])
                else:
                    B.cop("act", lambda e: e.activation(out=acc.ap, in_=acc.ap, func=AF.Silu), r=[acc], w=[acc])
                    B.cop("act", lambda e, o3=o3: e.activation(out=o3, in_=acc.ap, func=AF.Copy, scale=1.0 / 16.0),
                          r=[acc], w=[o_])
            if typ == "convk":
                for g in range(8):
                    ps = nps()
                    psb = ps.ap.bitcast(BF16)
                    for i in range(4):
                        blk = g * 4 + i
                        B.cop("pe", lambda e, psb=psb, i=i, blk=blk, o_=o_: e.transpose(
                            out=psb[:, i * 128:(i + 1) * 128], in_=o_.ap[:, blk * 128:(blk + 1) * 128], identity=id_b),
                            r=[o_, cb], w=[ps] if i == 0 else (), pw=[ps] if i else ())
                    eng = "dve" if g % 2 == 0 else "act"
                    dst = ktm.ap[:, g * 4:(g + 1) * 4, :]
                    src = psb[:, 0:512].rearrange("p (a b) -> p a b", b=128)
                    if eng == "dve":
                        B.cop("dve", lambda e, dst=dst, src=src: e.tensor_copy(out=dst, in_=src),
                              r=[ps], w=[ktm] if g == 0 else (), pw=[ktm] if g else ())
                    else:
                        B.cop("act", lambda e, dst=dst, src=src: e.activation(out=dst, in_=src, func=AF.Copy),
                              r=[ps], pw=[ktm])
                B.dma("sp", KA.ap[P * 32:(P + 1) * 32, :, cc * 128:(cc + 1) * 128].rearrange("b s c -> s b c"),
                      ktm.ap, ktm, r=[ktm], pw=[KA])
            if isconv:
                B.dma("sp", dest.ap[cc, :, tok0:tok0 + OWN], o_.ap, o_, r=[o_], pw=[dest])
            yield

    def tm_group(P, col0, dest, half, wslot=None, act_only=False):
        wb_ = load_weights(col0, wslot=wslot)
        for g in range(8):
            v_ = vst[cnt["vst"] % 2]
            cnt["vst"] += 1
            for i in range(4):
                blk = g * 4 + i
                ps = nps()
                for c in range(8):
                    B.cop("pe", lambda e, ps=ps, c=c, blk=blk: e.matmul(
                        ps.ap, lhsT=hn.ap[:, c, blk * 128:(blk + 1) * 128], rhs=wb_.ap[:, c, :],
                        start=(c == 0), stop=(c == 7)),
                        r=[wb_, hn_t[blk // 4]], w=[ps] if c == 0 else (), pw=[ps] if c else ())
                if i % 2 == 0 and not act_only:
                    B.cop("dve", lambda e, ps=ps, i=i, v_=v_: e.tensor_copy(out=v_.ap[:, i, :], in_=ps.ap),
                          r=[ps], w=[v_] if i == 0 else (), pw=[v_] if i else ())
                else:
                    B.cop("act", lambda e, ps=ps, i=i, v_=v_: e.activation(out=v_.ap[:, i, :], in_=ps.ap, func=AF.Copy),
                          r=[ps], w=[v_] if i == 0 else (), pw=[v_] if i else ())
            B.dma("sp", dest.ap[P * 32 + g * 4:P * 32 + g * 4 + 4, :, half * 512:(half + 1) * 512].rearrange(
                "b s c -> s b c"), v_.ap, v_, r=[v_], pw=[dest])
            yield

    def gates(P):
        B.dma("pool", wgb.ap, WIN[:, C_IG:C_IG + 8].rearrange("(c p) n -> p c n", p=128), wgb, w=[wgb])
        ps = nps()
        first = True
        for blk in range(32):
            for c in range(8):
                B.cop("pe", lambda e, c=c, blk=blk, ps=ps: e.matmul(
                    ps.ap[:, blk * 8:(blk + 1) * 8], lhsT=hn.ap[:, c, blk * 128:(blk + 1) * 128], rhs=wgb.ap[:, c, :],
                    start=(c == 0), stop=(c == 7)),
                    r=[wgb, hn_t[blk // 4]], w=[ps] if first else (), pw=() if first else [ps])
                first = False
        p3 = ps.ap[:, 0:256].rearrange("p (a b) -> p a b", b=8)
        gsl = GT.ap[:, P * 32:(P + 1) * 32, :]
        B.cop("dve", lambda e: e.tensor_tensor(
            out=gsl[:, :, 0:4], in0=p3[:, :, 0:4],
            in1=par[:, P_IB:P_IB + 4].unsqueeze(1).to_broadcast([128, 32, 4]), op=ALU.add),
            r=[ps, par], pw=[GT])
        B.cop("dve", lambda e: e.tensor_tensor(
            out=gtmp.ap, in0=p3[:, :, 4:8],
            in1=par[:, P_FB:P_FB + 4].unsqueeze(1).to_broadcast([128, 32, 4]), op=ALU.add),
            r=[ps, par], w=[gtmp])
        B.cop("act", lambda e: e.activation(out=gtmp.ap, in_=gtmp.ap, func=AF.Exp, scale=-1.0), r=[gtmp], w=[gtmp])
        B.cop("act", lambda e: e.activation(out=gtmp.ap, in_=gtmp.ap, func=AF.Ln, bias=1.0), r=[gtmp], w=[gtmp])
        B.cop("dve", lambda e: e.tensor_scalar(out=gsl[:, :, 4:8], in0=gtmp.ap, scalar1=-1.0, scalar2=None,
                                               op0=ALU.mult), r=[gtmp], pw=[GT])

    import itertools

    def run(g):
        for _ in g:
            pass

    def interleave(conv, fill, nfill, two_part=False):
        n = 0
        for _ in conv:
            if (not two_part) or n % 2 == 0:
                for _k in range(nfill):
                    next(fill, None)
            n += 1
        run(fill)

    for P in range(3):
        normalize(P)
        if P < 2:
            conv = itertools.chain(fm_group(P, C_KA, "convk", 0, KTA, wslot="c"),
                                   fm_group(P, C_KA + 512, "convk", 4, KTA, wslot="c"))
            fill = itertools.chain(tm_group(P, C_VA, VA, 0, wslot="f", act_only=True),
                                   tm_group(P, C_VA + 512, VA, 1, wslot="f", act_only=True),
                                   tm_group(P, C_VB, VB, 0, wslot="f", act_only=True),
                                   tm_group(P, C_VB + 512, VB, 1, wslot="f", act_only=True))
            interleave(conv, fill, 4, two_part=True)
            gates(P)
            run(fm_group(P, C_KB, "qkn", 0, KTB, gain_col=P_GK))
            run(fm_group(P, C_KB + 512, "qkn", 4, KTB, gain_col=P_GK))
        else:
            conv = itertools.chain(fm_group(P, C_QA, "convq", 0, QTA, wslot="c"),
                                   fm_group(P, C_QA + 512, "convq", 4, QTA, wslot="c"))
            fl = []
            for (c, typ, dst) in ((C_OA, "sig", OT), (C_ZA, "silu", ZAT), (C_ZB, "silu", ZBT), (C_GA, "sig", GAT),
                                  (C_GB, "sig", GBT)):
                fl.append((c, typ, 0, dst))
                fl.append((c + 512, typ, 4, dst))
            fill = itertools.chain(*[fm_group(P, c, typ, cc0, dst, wslot="f") for (c, typ, cc0, dst) in fl[:6]])
            interleave(conv, fill, 3, two_part=True)
            for n, (c, typ, cc0, dst) in enumerate(fl[6:]):
                run(fm_group(P, c, typ, cc0, dst))
            run(fm_group(P, C_QB, "qkn", 0, QTB, gain_col=P_GQ))
            run(fm_group(P, C_QB + 512, "qkn", 4, QTB, gain_col=P_GQ))
    if debug:
        B.dma("pool", GTD.ap, GT.ap, GT, r=[GT], pw=[GTD])

    B.barrier()
    B.release(base_mark)

    if stop_after != "B":
        mab = B.alloc("mab", [128, 8, 512], BF16)
        B.dma("sp", mab.ap, MAD, mab, w=[mab])
        kt = [B.alloc(f"kt{i}", [128, S], BF16) for i in range(2)]
        vv = [B.alloc(f"vv{i}", [128, NBLK, 128], BF16) for i in range(2)]
        qt = [B.alloc(f"qt{i}", [128, OWN], BF16) for i in range(2)]
        zb = [B.alloc(f"zb{i}", [128, OWN], BF16) for i in range(2)]
        hbst = [B.alloc(f"hbst{i}", [128, OWN], BF16) for i in range(2)]
        NW = 3
        eb = [[B.alloc(f"e{s}{i}", [128, 512], BF16) for i in range(NW)] for s in range(2)]
        lb = [[B.alloc(f"l{s}{i}", [128, 512], BF16) for i in range(NW)] for s in range(2)]
        e2 = [[B.alloc(f"x{s}{i}", [128, 512], BF16) for i in range(NW)] for s in range(2)]
        wt = [[B.alloc(f"w{s}{i}", [128, 512], BF16) for i in range(NW)] for s in range(2)]
        def attn_loads(h):
            k = h % 2
            kt_, vv_, qt_, zb_ = kt[k], vv[k], qt[k], zb[k]
            B.dma("sp", kt_.ap, KTB.ap[h], kt_, r=[KTB], w=[kt_])
            B.dma("sp", qt_.ap, QTB.ap[h], qt_, r=[QTB], w=[qt_])
            B.dma("sp", vv_.ap, VB.ap[:, :, h * 128:(h + 1) * 128].rearrange("b s c -> s b c"), vv_, r=[VB], w=[vv_])
            B.dma("sp", zb_.ap, ZBT.ap[h], zb_, r=[ZBT], w=[zb_])

        def attn_head(h):
            k = h % 2
            kt_, vv_, qt_, zb_, hb_ = kt[k], vv[k], qt[k], zb[k], hbst[k]
            if h + 1 < 8:
                attn_loads(h + 1)
            streams = []
            for s, Is in enumerate(([7, 0, 5, 2], [6, 1, 4, 3])):
                units = []
                for I in Is:
                    n = 8 * I + 8
                    for u in range(n):
                        units.append((I, n - 1 - u, u == 0, u == n - 1))
                streams.append(units)
            nun = len(streams[0])
            assert nun == len(streams[1])
            first_hb = [True]

            def cm(I, kb):
                return 128 * ((kb - 8 * I) // 2) if kb >= 8 * I else 0

            def qk(s, ui):
                I, kb, fst, lst = streams[s][ui]
                c = cm(I, kb)
                z = PS[2 * s + ui % 2]
                B.cop("pe", lambda e: e.matmul(z.ap[:, c:512], lhsT=kt_.ap[:, kb * 128:(kb + 1) * 128],
                                               rhs=qt_.ap[:, I * 512 + c:(I + 1) * 512], start=True, stop=True),
                      r=[kt_, qt_], w=[z])

            def st_exp(s, ui):
                I, kb, fst, lst = streams[s][ui]
                c = cm(I, kb)
                z = PS[2 * s + ui % 2]
                e_ = eb[s][ui % NW]
                B.cop("act", lambda e: e.activation(out=e_.ap[:, c:512], in_=z.ap[:, c:512], func=AF.Exp), r=[z], w=[e_])
                if kb >= 8 * I:
                    m = kb - 8 * I
                    B.cop("dve", lambda e: e.tensor_tensor(out=e_.ap[:, c:512], in0=e_.ap[:, c:512],
                                                           in1=mab.ap[:, m, c:512], op=ALU.mult),
                          r=[e_, mab], w=[e_])

            def st_ln(s, ui):
                I, kb, fst, lst = streams[s][ui]
                c = cm(I, kb)
                e_ = eb[s][ui % NW]
                l_ = lb[s][ui % NW]
                B.cop("act", lambda e: e.activation(out=l_.ap[:, c:512], in_=e_.ap[:, c:512], func=AF.Ln, bias=1.0),
                      r=[e_], w=[l_])

            def st_tri(s, ui):
                I, kb, fst, lst = streams[s][ui]
                c = cm(I, kb)
                l_ = lb[s][ui % NW]
                sp_ = PS[4 + s]
                B.cop("pe", lambda e: e.matmul(sp_.ap[:, c:512], lhsT=tri_b, rhs=l_.ap[:, c:512], start=fst, stop=True,
                                               skip_group_check=not fst), r=[l_, cb], w=[sp_])

            def st_exp2(s, ui):
                I, kb, fst, lst = streams[s][ui]
                c = cm(I, kb)
                sp_ = PS[4 + s]
                x_ = e2[s][ui % NW]
                B.cop("act", lambda e: e.activation(out=x_.ap[:, c:512], in_=sp_.ap[:, c:512], func=AF.Exp),
                      r=[sp_], w=[x_])

            def st_rest(s, ui):
                I, kb, fst, lst = streams[s][ui]
                if lst:
                    return
                c = cm(I, kb)
                l_ = lb[s][ui % NW]
                sp_ = PS[4 + s]
                B.cop("pe", lambda e: e.matmul(sp_.ap[:, c:512], lhsT=rest_b, rhs=l_.ap[:, c:512], start=False, stop=True,
                                               skip_group_check=True), r=[l_, cb], w=[sp_])

            def st_mul(s, ui):
                I, kb, fst, lst = streams[s][ui]
                c = cm(I, kb)
                e_ = eb[s][ui % NW]
                x_ = e2[s][ui % NW]
                w_ = wt[s][ui % NW]
                B.cop("dve", lambda e: e.tensor_tensor(out=w_.ap[:, c:512], in0=e_.ap[:, c:512], in1=x_.ap[:, c:512],
                                                       op=ALU.mult), r=[e_, x_], w=[w_])

            def st_pv(s, ui):
                I, kb, fst, lst = streams[s][ui]
                c = cm(I, kb)
                w_ = wt[s][ui % NW]
                op_ = PS[6 + s]
                B.cop("pe", lambda e: e.matmul(op_.ap[:, c:512], lhsT=vv_.ap[:, kb, :], rhs=w_.ap[:, c:512], start=fst,
                                               stop=lst, skip_group_check=not fst),
                      r=[vv_, w_], w=[op_] if fst else (), pw=() if fst else [op_])
                if lst:
                    B.cop("dve", lambda e: e.tensor_tensor(out=hb_.ap[:, I * 512:(I + 1) * 512], in0=op_.ap,
                                                           in1=zb_.ap[:, I * 512:(I + 1) * 512], op=ALU.mult),
                          r=[op_, zb_], w=[hb_] if first_hb[0] else (), pw=() if first_hb[0] else [hb_])
                    first_hb[0] = False

            for s in range(2):
                qk(s, 0)
            for s in range(2):
                qk(s, 1)
            for s in range(2):
                st_exp(s, 0)
            for ui in range(nun):
                for s in range(2):
                    st_ln(s, ui)
                for s in range(2):
                    st_tri(s, ui)
                if ui + 1 < nun:
                    for s in range(2):
                        st_exp(s, ui + 1)
                if ui + 2 < nun:
                    for s in range(2):
                        qk(s, ui + 2)
                for s in range(2):
                    st_exp2(s, ui)
                for s in range(2):
                    st_rest(s, ui)
                for s in range(2):
                    st_mul(s, ui)
                for s in range(2):
                    st_pv(s, ui)
            B.dma("pool", HB.ap[h], hb_.ap, hb_, r=[hb_], pw=[HB])
        attn_loads(0)
        for h in range(8):
            attn_head(h)
        B.barrier()
        B.release(base_mark)

    if stop_after not in ("B", "C"):
        ewts = []
        for nm, Wd in (("wa", WA), ("wb", WB), ("wo", WO)):
            wt_ = B.alloc(nm, [128, 8, 1024], BF16)
            for hf in range(2):
                B.dma("pool", wt_.ap[:, :, hf * 512:(hf + 1) * 512],
                      Wd[:, hf * 512:(hf + 1) * 512].rearrange("(c p) n -> p c n", p=128), wt_,
                      w=[wt_] if hf == 0 else (), pw=[wt_] if hf else ())
            ewts.append(wt_)
        e_mark = B.mark()
        mmf = B.alloc("mmf", [128, 2, 128], F32)
        mmb = B.alloc("mmb", [128, 2, 128], BF16)
        B.dma("sp", mmf.ap, MMD, mmf, w=[mmf])
        B.cop("dve", lambda e: e.tensor_copy(out=mmb.ap, in_=mmf.ap), r=[mmf], w=[mmb])
        hastS = [B.alloc(f"hast{i}", [128, 8, 128], BF16) for i in range(2)]
        C32 = B.alloc("c32", [128, 2, 4, 257], F32)
        Cbf = [B.alloc(f"cbf{i}", [128, 2, 4, 257], BF16) for i in range(2)]
        nbb = [B.alloc(f"nbb{i}", [128, 2, 4, 128], BF16) for i in range(2)]
        ktaL = [B.alloc(f"ktaL{i}", [128, 8, 256], BF16) for i in range(2)]
        kaL = [B.alloc(f"kaL{i}", [128, 2, 1024], BF16) for i in range(2)]
        vaL = [B.alloc(f"vaL{i}", [128, 8, 257], BF16) for i in range(2)]
        qtaL = [B.alloc(f"qtaL{i}", [128, 8, 128], BF16) for i in range(2)]
        otL = [B.alloc(f"otL{i}", [128, 8, 128], BF16) for i in range(2)]
        zaL = [B.alloc(f"zaL{i}", [128, 8, 128], BF16) for i in range(2)]
        lfbS = [B.alloc(f"lfb{i}", [128, 8, 128], F32) for i in range(2)]
        cbDS = [B.alloc(f"cbd{i}", [128, 8], F32) for i in range(2)]
        waDS = [B.alloc(f"wad{i}", [128, 8], F32) for i in range(2)]
        waBS = [B.alloc(f"wab{i}", [128, 8], BF16) for i in range(2)]
        decS = [B.alloc(f"dec{i}", [128, 4], F32) for i in range(2)]
        vwS = [B.alloc(f"vw{i}", [128, 8, 257], BF16) for i in range(2)]
        NS = 4
        ebD = [B.alloc(f"ebd{i}", [128, 128], F32) for i in range(NS)]
        qs = [B.alloc(f"qs{i}", [128, 2, 128], BF16) for i in range(NS)]
        dt_ = [B.alloc(f"dt{i}", [128, 2, 128], F32) for i in range(NS)]
        ptm = [B.alloc(f"ptm{i}", [128, 2, 128], BF16) for i in range(NS)]
        pt = [B.alloc(f"pt{i}", [128, 2, 128], BF16) for i in range(NS)]
        rr = [B.alloc(f"rr{i}", [128, 128], F32) for i in range(NS)]
        u1 = [B.alloc(f"u1{i}", [128, 2, 128], F32) for i in range(NS)]
        uu = [B.alloc(f"uu{i}", [128, 2, 128], F32) for i in range(NS)]
        usq = [B.alloc(f"usq{i}", [128, 2, 128], BF16) for i in range(NS)]
        tD = [B.alloc(f"td{i}", [128, 128], F32) for i in range(NS)]
        rsD = [B.alloc(f"rsd{i}", [128, 128], F32) for i in range(NS)]
        g1 = [B.alloc(f"g1{i}", [128, 2, 128], F32) for i in range(NS)]
        nsb = [B.alloc(f"nsb{i}", [128, 2], F32) for i in range(NS)]
        B.cop("dve", lambda e: e.memset(C32.ap, 0.0), w=[C32])
        B.cop("pool", lambda e: e.memset(Cbf[0].ap, 0.0), w=[Cbf[0]])
        B.cop("pool", lambda e: e.memset(nbb[0].ap, 0.0), w=[nbb[0]])
        for i in range(2):
            B.cop("pool", lambda e, i=i: e.memset(vaL[i].ap, 1.0), w=[vaL[i]])
        def sc_loads(i):
            k = i % 2
            kta_, ka_, va_, qta_, ot_, za_ = ktaL[k], kaL[k], vaL[k], qtaL[k], otL[k], zaL[k]
            B.dma("sp", kta_.ap, KTA.ap[:, :, i * 256:(i + 1) * 256].rearrange("c p s -> p c s"), kta_, r=[KTA], w=[kta_])
            B.dma("sp", ka_.ap, KA.ap[2 * i:2 * i + 2].rearrange("b s c -> s b c"), ka_, r=[KA], w=[ka_])
            for sc in range(2):
                B.dma("sp", va_.ap[:, sc * 4:(sc + 1) * 4, 0:256],
                      VA.ap[2 * i + sc].rearrange("s (h v) -> s h v", v=256), va_, r=[VA], pw=[va_])
            B.dma("sp", qta_.ap, QTA.ap[:, :, i * 128:(i + 1) * 128].rearrange("c p s -> p c s"), qta_, r=[QTA], w=[qta_])
            B.dma("sp", ot_.ap, OT.ap[:, :, i * 128:(i + 1) * 128].rearrange("c p s -> p c s"), ot_, r=[OT], w=[ot_])
            B.dma("sp", za_.ap, ZAT.ap[:, :, i * 128:(i + 1) * 128].rearrange("c p s -> p c s"), za_, r=[ZAT], w=[za_])

        def superchunk(i):
            k = i % 2
            kta_, ka_, va_, qta_, ot_, za_ = ktaL[k], kaL[k], vaL[k], qtaL[k], otL[k], zaL[k]
            if i + 1 < 32:
                sc_loads(i + 1)
            cprev, cnext = Cbf[i % 2], Cbf[(i + 1) % 2]
            nprev, nnext = nbb[i % 2], nbb[(i + 1) % 2]
            lfb_, cbD_, waD_, waB_, dec_, vw_ = lfbS[k], cbDS[k], waDS[k], waBS[k], decS[k], vwS[k]
            hast_ = hastS[k]
            lf_in = GT.ap[:, 2 * i:2 * i + 2, 4:8]
            ig_in = GT.ap[:, 2 * i:2 * i + 2, 0:4]
            B.cop("dve", lambda e: e.tensor_copy(
                out=lfb_.ap.rearrange("p (a b) s -> p a b s", b=4),
                in_=lf_in.unsqueeze(3).to_broadcast([128, 2, 4, 128])), r=[GT], w=[lfb_])
            psB = PS[1]
            psS = psB.ap[:, 400:412]
            B.cop("pe", lambda e: e.matmul(psS[:, 0:4], lhsT=tri_f, rhs=GT.ap[:, 2 * i, 4:8],
                                           start=True, stop=True), r=[GT, cf], w=[psB])
            B.cop("pe", lambda e: e.matmul(psS[:, 4:8], lhsT=ones_f, rhs=GT.ap[:, 2 * i, 4:8],
                                           start=True, stop=False), r=[GT, cf], pw=[psB])
            B.cop("pe", lambda e: e.matmul(psS[:, 4:8], lhsT=tri_f, rhs=GT.ap[:, 2 * i + 1, 4:8],
                                           start=False, stop=True), r=[GT, cf], pw=[psB])
            B.cop("pe", lambda e: e.matmul(psS[:, 8:12], lhsT=ones_f, rhs=GT.ap[:, 2 * i, 4:8],
                                           start=True, stop=False), r=[GT, cf], pw=[psB])
            B.cop("pe", lambda e: e.matmul(psS[:, 8:12], lhsT=ones_f, rhs=GT.ap[:, 2 * i + 1, 4:8],
                                           start=False, stop=True), r=[GT, cf], pw=[psB])
            B.cop("dve", lambda e: e.scalar_tensor_tensor(
                out=cbD_.ap.rearrange("p (a b) -> p a b", b=4), in0=psS[:, 0:8].rearrange("p (a b) -> p a b", b=4),
                scalar=-1.0, in1=ig_in, op0=ALU.mult, op1=ALU.add), r=[psB, GT], w=[cbD_])
            B.cop("dve", lambda e: e.tensor_tensor(
                out=waD_.ap.rearrange("p (a b) -> p a b", b=4), in0=cbD_.ap.rearrange("p (a b) -> p a b", b=4),
                in1=psS[:, 8:12].unsqueeze(1).to_broadcast([128, 2, 4]), op=ALU.add), r=[psB, cbD_], w=[waD_])
            B.cop("act", lambda e: e.activation(out=waD_.ap, in_=waD_.ap, func=AF.Exp), r=[waD_], w=[waD_])
            B.cop("act", lambda e: e.activation(out=dec_.ap, in_=psS[:, 8:12], func=AF.Exp), r=[psB], w=[dec_])
            B.cop("dve", lambda e: e.tensor_copy(out=waB_.ap, in_=waD_.ap), r=[waD_], w=[waB_])
            B.cop("pool", lambda e: e.tensor_tensor(
                out=vw_.ap, in0=va_.ap, in1=waD_.ap.unsqueeze(2).to_broadcast([128, 8, 257]), op=ALU.mult),
                r=[va_, waD_], w=[vw_])

            def headpart(h):
                kk = h
                bX, bY = PS[2 * h], PS[2 * h + 1]
                eb_, qs_, dt1, ptm_, pt_ = ebD[kk], qs[kk], dt_[kk], ptm[kk], pt[kk]
                r_, u1_, u_, usq_, t_, rs_, g1_, nsb_ = rr[kk], u1[kk], uu[kk], usq[kk], tD[kk], rsD[kk], g1[kk], nsb[kk]
                for jc in range(2):
                    B.cop("pe", lambda e, jc=jc: e.matmul(
                        bX.ap[:, 256:384], lhsT=lfb_.ap[:, jc * 4 + h, :], rhs=mmf.ap[:, jc, :],
                        start=(jc == 0), stop=(jc == 1)), r=[lfb_, mmf], w=[bX] if jc == 0 else (), pw=[bX] if jc else ())
                for sc in range(2):
                    for kc in range(2):
                        B.cop("pe", lambda e, sc=sc, kc=kc: e.matmul(
                            bX.ap[:, sc * 128:(sc + 1) * 128], lhsT=kta_.ap[:, h * 2 + kc, sc * 128:(sc + 1) * 128],
                            rhs=qta_.ap[:, h * 2 + kc, :], start=(kc == 0), stop=(kc == 1)),
                            r=[kta_, qta_], pw=[bX])
                yield
                B.cop("act", lambda e: e.activation(out=eb_.ap, in_=bX.ap[:, 256:384], func=AF.Exp), r=[bX], w=[eb_])
                for sc in range(2):
                    B.cop("act", lambda e, sc=sc: e.activation(
                        out=dt1.ap[:, sc, :], in_=bX.ap[:, 256:384], func=AF.Exp,
                        bias=cbD_.ap[:, sc * 4 + h:sc * 4 + h + 1]), r=[bX, cbD_], w=[dt1] if sc == 0 else (),
                        pw=[dt1] if sc else ())
                B.cop("dve", lambda e: e.tensor_tensor(
                    out=qs_.ap, in0=qta_.ap[:, h * 2:h * 2 + 2, :],
                    in1=eb_.ap.unsqueeze(1).to_broadcast([128, 2, 128]), op=ALU.mult), r=[qta_, eb_], w=[qs_])
                yield
                B.cop("dve", lambda e: e.tensor_tensor(
                    out=ptm_.ap, in0=bX.ap[:, 0:256].rearrange("p (a b) -> p a b", b=128), in1=dt1.ap, op=ALU.mult),
                    r=[bX, dt1], w=[ptm_])
                B.cop("pool", lambda e: e.tensor_tensor(out=pt_.ap, in0=ptm_.ap, in1=mmb.ap, op=ALU.mult),
                      r=[ptm_, mmb], w=[pt_])
                yield
                first = True
                for vc in range(2):
                    for sc in range(2):
                        B.cop("pe", lambda e, vc=vc, sc=sc: e.matmul(
                            bY.ap[:, vc * 128:(vc + 1) * 128], lhsT=va_.ap[:, sc * 4 + h, vc * 128:(vc + 1) * 128],
                            rhs=pt_.ap[:, sc, :], start=(sc == 0), stop=False),
                            r=[va_, pt_], w=[bY] if first else (), pw=() if first else [bY])
                        first = False
                    for kc in range(2):
                        B.cop("pe", lambda e, vc=vc, kc=kc: e.matmul(
                            bY.ap[:, vc * 128:(vc + 1) * 128], lhsT=cprev.ap[:, kc, h, vc * 128:(vc + 1) * 128],
                            rhs=qs_.ap[:, kc, :], start=False, stop=(kc == 1)), r=[cprev, qs_], pw=[bY])
                for sc in range(2):
                    B.cop("pe", lambda e, sc=sc: e.matmul(
                        bX.ap[:, 384:512], lhsT=ones_b, rhs=pt_.ap[:, sc, :], start=(sc == 0), stop=False),
                        r=[pt_, cb], w=[bX] if sc == 0 else (), pw=[bX] if sc else ())
                for kc in range(2):
                    B.cop("pe", lambda e, kc=kc: e.matmul(
                        bX.ap[:, 384:512], lhsT=nprev.ap[:, kc, h, :], rhs=qs_.ap[:, kc, :], start=False,
                        stop=(kc == 1)), r=[nprev, qs_], pw=[bX])
                for kc in range(2):
                    for sc in range(2):
                        B.cop("pe", lambda e, kc=kc, sc=sc: e.matmul(
                            bY.ap[:, 384 + kc:385 + kc],
                            lhsT=ka_.ap[:, sc, h * 256 + kc * 128:h * 256 + (kc + 1) * 128],
                            rhs=waB_.ap[:, sc * 4 + h:sc * 4 + h + 1], start=(sc == 0), stop=(sc == 1)),
                            r=[ka_, waB_], pw=[bY])
                yield
                B.cop("dve", lambda e: e.tensor_scalar(out=r_.ap, in0=bX.ap[:, 384:512], scalar1=-1.0,
                                                       scalar2=None, op0=ALU.mult), r=[bX], w=[r_])
                B.cop("dve", lambda e: e.scalar_tensor_tensor(out=r_.ap, in0=r_.ap, scalar=1.0,
                                                              in1=bX.ap[:, 384:512], op0=ALU.max, op1=ALU.max),
                      r=[bX, r_], w=[r_])
                B.cop("dve", lambda e: e.tensor_tensor(
                    out=u1_.ap, in0=bY.ap[:, 0:256].rearrange("p (a b) -> p a b", b=128),
                    in1=ot_.ap[:, h * 2:h * 2 + 2, :], op=ALU.mult), r=[bY, ot_], w=[u1_])
                B.cop("act", lambda e: e.activation(out=r_.ap, in_=r_.ap, func=AF.Ln), r=[r_], w=[r_])
                B.cop("act", lambda e: e.activation(out=r_.ap, in_=r_.ap, func=AF.Exp, scale=-1.0), r=[r_], w=[r_])
                B.cop("act", lambda e: e.activation(out=nsb_.ap, in_=bY.ap[:, 384:386], func=AF.Copy), r=[bY], w=[nsb_])
                B.cop("dve", lambda e: e.scalar_tensor_tensor(
                    out=C32.ap[:, :, h, 256], in0=C32.ap[:, :, h, 256], scalar=dec_.ap[:, h:h + 1],
                    in1=nsb_.ap, op0=ALU.mult, op1=ALU.add), r=[nsb_, dec_, C32], w=[C32])
                B.cop("pool", lambda e: e.tensor_copy(
                    out=nnext.ap[:, :, h, :], in_=C32.ap[:, :, h, 256:257].to_broadcast([128, 2, 128])),
                    r=[C32], w=[nnext] if h == 0 else (), pw=[nnext] if h else ())
                yield
                B.cop("dve", lambda e: e.tensor_tensor(
                    out=u_.ap, in0=u1_.ap, in1=r_.ap.unsqueeze(1).to_broadcast([128, 2, 128]), op=ALU.mult),
                    r=[u1_, r_], w=[u_])
                B.cop("act", lambda e: e.activation(out=usq_.ap, in_=u_.ap, func=AF.Square), r=[u_], w=[usq_])
                yield
                for (kc, bk) in ((0, bX), (1, bY)):
                    for sc in range(2):
                        B.cop("pe", lambda e, kc=kc, sc=sc, bk=bk: e.matmul(
                            bk.ap[:, 0:256],
                            lhsT=ka_.ap[:, sc, h * 256 + kc * 128:h * 256 + (kc + 1) * 128],
                            rhs=vw_.ap[:, sc * 4 + h, 0:256], start=(sc == 0), stop=(sc == 1)),
                            r=[ka_, vw_], w=[bk] if sc == 0 else (), pw=[bk] if sc else ())
                for vc in range(2):
                    B.cop("pe", lambda e, vc=vc: e.matmul(
                        bY.ap[:, 256:384], lhsT=ones_b, rhs=usq_.ap[:, vc, :], start=(vc == 0), stop=(vc == 1)),
                        r=[usq_, cb], pw=[bY])
                yield
                B.cop("act", lambda e: e.activation(out=t_.ap, in_=bY.ap[:, 256:384], func=AF.Ln,
                                                    scale=1.0 / 256, bias=EPS), r=[bY], w=[t_])
                B.cop("act", lambda e: e.activation(out=rs_.ap, in_=t_.ap, func=AF.Exp, scale=-0.5),
                      r=[t_], w=[rs_])
                for (kc, bk) in ((0, bX), (1, bY)):
                    B.cop("dve", lambda e, kc=kc, bk=bk: e.scalar_tensor_tensor(
                        out=C32.ap[:, kc, h, 0:256], in0=C32.ap[:, kc, h, 0:256], scalar=dec_.ap[:, h:h + 1],
                        in1=bk.ap[:, 0:256], op0=ALU.mult, op1=ALU.add),
                        r=[bk, dec_, C32], w=[C32])
                yield
                B.cop("act", lambda e: e.activation(out=cnext.ap[:, :, h, :], in_=C32.ap[:, :, h, :], func=AF.Copy),
                      r=[C32], w=[cnext] if h == 0 else (), pw=[cnext] if h else ())
                for vc in range(2):
                    B.cop("dve", lambda e, vc=vc: e.scalar_tensor_tensor(
                        out=g1_.ap[:, vc, :], in0=u_.ap[:, vc, :],
                        scalar=par[:, P_AHG + h * 2 + vc:P_AHG + h * 2 + vc + 1], in1=rs_.ap,
                        op0=ALU.mult, op1=ALU.mult), r=[u_, rs_, par], w=[g1_] if vc == 0 else (),
                        pw=[g1_] if vc else ())
                B.cop("pool", lambda e: e.tensor_tensor(
                    out=hast_.ap[:, h * 2:h * 2 + 2, :], in0=g1_.ap,
                    in1=za_.ap[:, h * 2:h * 2 + 2, :], op=ALU.mult), r=[g1_, za_],
                    w=[hast_] if h == 0 else (), pw=[hast_] if h else ())

            gens = [headpart(h) for h in range(4)]
            live = list(gens)
            while live:
                for g in list(live):
                    try:
                        next(g)
                    except StopIteration:
                        live.remove(g)
            B.dma("sp", HA.ap[:, :, i * 128:(i + 1) * 128].rearrange("c p t -> p c t"), hast_.ap, hast_,
                  r=[hast_], pw=[HA])
        sc_loads(0)
        for i in range(32):
            superchunk(i)
        B.barrier()
        B.release(e_mark)

    if stop_after is None:
        st32e = [B.alloc(f"st32e{i}", [128, 8, 512], F32) for i in range(2)]
        wa_, wb2_, wo_ = ewts
        haL = [B.alloc(f"haL{i}", [128, 8, 512], BF16) for i in range(2)]
        hbL = [B.alloc(f"hbL{i}", [128, 8, 512], BF16) for i in range(2)]
        gaL = [B.alloc(f"gaL{i}", [128, 8, 512], BF16) for i in range(2)]
        gbL = [B.alloc(f"gbL{i}", [128, 8, 512], BF16) for i in range(2)]
        xo = st32e
        mrg = B.alloc("mrg", [128, 8, 512], BF16)
        m1 = [B.alloc(f"m1{i}", [128, 512], F32) for i in range(2)]
        m2 = [B.alloc(f"m2{i}", [128, 512], F32) for i in range(2)]
        mc = 0
        for j in range(8):
            k = j % 2
            sl = slice(j * 512, (j + 1) * 512)
            for (L, Sx) in ((haL[k], HA), (hbL[k], HB), (gaL[k], GAT), (gbL[k], GBT)):
                B.dma("sp", L.ap, Sx.ap[:, :, sl].rearrange("c p t -> p c t"), L, r=[Sx], w=[L])
            xo_ = xo[k]
            B.dma("sp", xo_.ap, XT[2, :, sl].rearrange("(c p) t -> p c t", p=128), xo_, w=[xo_])
            for ec in range(8):
                pa, pb = nps(), nps()
                for c in range(8):
                    B.cop("pe", lambda e, pa=pa, c=c, ec=ec, k=k: e.matmul(
                        pa.ap, lhsT=wa_.ap[:, c, ec * 128:(ec + 1) * 128], rhs=haL[k].ap[:, c, :],
                        start=(c == 0), stop=(c == 7)), r=[wa_, haL[k]], w=[pa] if c == 0 else (), pw=[pa] if c else ())
                for c in range(8):
                    B.cop("pe", lambda e, pb=pb, c=c, ec=ec, k=k: e.matmul(
                        pb.ap, lhsT=wb2_.ap[:, c, ec * 128:(ec + 1) * 128], rhs=hbL[k].ap[:, c, :],
                        start=(c == 0), stop=(c == 7)), r=[wb2_, hbL[k]], w=[pb] if c == 0 else (), pw=[pb] if c else ())
                a1, a2 = m1[mc % 2], m2[mc % 2]
                mc += 1
                B.cop("dve", lambda e, pa=pa, a1=a1, ec=ec, k=k: e.tensor_tensor(out=a1.ap, in0=pa.ap, in1=gaL[k].ap[:, ec, :],
                                                                               op=ALU.mult), r=[pa, gaL[k]], w=[a1])
                B.cop("dve", lambda e, pb=pb, a2=a2, ec=ec, k=k: e.tensor_tensor(out=a2.ap, in0=pb.ap, in1=gbL[k].ap[:, ec, :],
                                                                               op=ALU.mult), r=[pb, gbL[k]], w=[a2])
                B.cop("pool", lambda e, a1=a1, a2=a2, ec=ec: e.tensor_tensor(out=mrg.ap[:, ec, :], in0=a1.ap, in1=a2.ap,
                                                                             op=ALU.add), r=[a1, a2],
                      w=[mrg] if ec == 0 else (), pw=[mrg] if ec else ())
            for ec in range(8):
                po = nps()
                for c in range(8):
                    B.cop("pe", lambda e, po=po, c=c, ec=ec: e.matmul(
                        po.ap, lhsT=wo_.ap[:, c, ec * 128:(ec + 1) * 128], rhs=mrg.ap[:, c, :],
                        start=(c == 0), stop=(c == 7)), r=[wo_, mrg], w=[po] if c == 0 else (), pw=[po] if c else ())
                B.cop("dve", lambda e, po=po, ec=ec, xo_=xo_: e.tensor_tensor(out=xo_.ap[:, ec, :], in0=po.ap,
                                                                             in1=xo_.ap[:, ec, :], op=ALU.add),
                      r=[po, xo_], pw=[xo_])
            B.dma("pool", OUT[:, sl].rearrange("(c p) t -> p c t", p=128), xo_.ap, xo_, r=[xo_])
    else:
        zt = B.alloc("zt", [128, 8, 512], F32)
        B.cop("dve", lambda e: e.memset(zt.ap, 0.0), w=[zt])
        for j in range(8):
            B.dma("pool", OUT[:, j * 512:(j + 1) * 512].rearrange("(c p) t -> p c t", p=128), zt.ap, zt, r=[zt])

    es = B.emit()
    return nc, B, es


def host_prep(core, x, norm_gain, w_in, conv_w, conv_b, i_bias, f_bias, a_head_gain, qk_gain_q, qk_gain_k,
              w_branch_a, w_branch_b, w_out):
    b, p = core // 2, core % 2
    xb = x[b]
    XTn = np.zeros((3, D, TOKW), np.float32)

    def halo(glist):
        h = np.zeros((len(glist) * 4, D), np.float32)
        for n, g in enumerate(glist):
            if g > 0:
                h[n * 4:(n + 1) * 4] = xb[128 * g - 4:128 * g]
        return h
    XTn[0, :, :OWN] = xb[0:OWN].T
    XTn[0, :, OWN:] = halo(list(range(0, 32))).T
    XTn[1, :, :OWN] = xb[OWN:].T
    XTn[1, :, OWN:] = halo(list(range(32, 64))).T
    own = [2 * i + p for i in range(32)]
    xo = xb.reshape(NBLK, 128, D)[own].reshape(OWN, D)
    XTn[2, :, :OWN] = xo.T
    XTn[2, :, OWN:] = halo(own).T

    par = np.zeros((128, NPAR), np.float32)
    par[:, P_GN:P_GN + 8] = norm_gain[0].reshape(8, 128).T
    cw = conv_w[0]
    par[:, P_CW:P_CW + 64] = cw.T.reshape(16, 128, 4).transpose(1, 0, 2).reshape(128, 64)
    par[:, P_CB:P_CB + 16] = conv_b[0].reshape(16, 128).T
    par[:, P_IB:P_IB + 4] = i_bias[0][None, :]
    par[:, P_FB:P_FB + 4] = f_bias[0][None, :]
    par[:, P_AHG:P_AHG + 8] = a_head_gain[0].reshape(8, 128).T
    par[:, P_GQ] = qk_gain_q[0]
    par[:, P_GK] = qk_gain_k[0]

    j = np.arange(128)[:, None]
    s = np.arange(128)[None, :]
    cf = np.zeros((128, NCF), np.float32)
    cf[:, CF_ONES:CF_ONES + 128] = 1.0
    cf[:, CF_TRI:CF_TRI + 128] = (j <= s)
    cf[:, CF_ID:CF_ID + 128] = np.eye(128)
    cbf = np.zeros((128, NCB), np.float32)
    cbf[:, CB_ONES:CB_ONES + 128] = 1.0
    cbf[:, CB_ID:CB_ID + 128] = np.eye(128)
    cbf[:, CB_TRI:CB_TRI + 128] = -1.0 * (j >= s)
    cbf[:, CB_REST:CB_REST + 128] = -1.0 * (j < s)
    ma = np.zeros((128, 8, 512), np.float32)
    for m in range(8):
        for a in range(4):
            kpos = m * 128 + np.arange(128)[:, None]
            qpos = (2 * a + p) * 128 + np.arange(128)[None, :]
            ma[:, m, a * 128:(a + 1) * 128] = (kpos < qpos)
    mm = np.zeros((128, 2, 128), np.float32)
    for jc in range(2):
        kpos = jc * 128 + np.arange(128)[:, None]
        qpos = p * 128 + np.arange(128)[None, :]
        mm[:, jc, :] = (kpos <= qpos)
    return {
        "xt": XTn, "w_in": np.ascontiguousarray(w_in[0]), "w_a": np.ascontiguousarray(w_branch_a[0]),
        "w_b": np.ascontiguousarray(w_branch_b[0]), "w_o": np.ascontiguousarray(w_out[0]),
        "params": par, "cf": cf, "cbf": cbf.astype(ml_dtypes.bfloat16), "ma": ma.astype(ml_dtypes.bfloat16),
        "mm": mm,
    }


_CACHE = {}


def kernel(**inputs):
    inputs = {k: np.asarray(v) for k, v in inputs.items()}
    if "nc" not in _CACHE:
        _CACHE["nc"] = build_program()
    nc, B, es = _CACHE["nc"]
    in_maps = [host_prep(c, **inputs) for c in range(8)]
    res = run_bass_kernel_spmd(nc, in_maps, core_ids=list(range(8)))
    out = np.zeros((4, S, D), np.float32)
    for c in range(8):
        b, p = c // 2, c % 2
        o = np.asarray(res.results[c]["out"]).T.reshape(32, 128, D)
        out[b].reshape(NBLK, 128, D)[p::2] = o
    return out
```

```python
import contextlib
import numpy as np
import ml_dtypes
import concourse.bass as bass
import concourse.mybir as mybir
from concourse.bass_utils import run_bass_kernel_spmd

F32 = mybir.dt.float32
BF16 = mybir.dt.bfloat16
U8 = mybir.dt.uint8
AF = mybir.ActivationFunctionType
ALU = mybir.AluOpType

S = 8192
D = 1024
NBLK = 64
OWN = 4096
HALO = 128
TOKW = OWN + HALO
INW = 11272
C_QA, C_KA, C_VA, C_OA, C_ZA, C_IG, C_QB, C_KB, C_VB, C_ZB, C_GA, C_GB = (
    0, 1024, 2048, 3072, 4096, 5120, 5128, 6152, 7176, 8200, 9224, 10248)
EPS = 1e-6

P_GN = 0
P_CW = 8
P_CB = 72
P_IB = 88
P_FB = 92
P_AHG = 96
P_GQ = 104
P_GK = 105
NPAR = 106
CF_ONES, CF_TRI, CF_ID = 0, 128, 256
NCF = 384
CB_ONES, CB_ID, CB_TRI, CB_REST = 0, 128, 256, 384
NCB = 512


class Tl:
    __slots__ = ("name", "w", "r", "sem", "dcount")

    def __init__(self, name):
        self.name = name
        self.w = {}
        self.r = {}
        self.sem = None
        self.dcount = 0


class Buf:
    __slots__ = ("ap", "t")

    def __init__(self, ap, name):
        self.ap = ap
        self.t = Tl(name)

    def __getitem__(self, k):
        return self.ap[k]


class Op:
    __slots__ = ("eng", "fn", "idx", "deps", "sig", "dma_t", "dval", "sigcount", "raw")

    def __init__(self, eng, fn):
        self.eng = eng
        self.fn = fn
        self.deps = None
        self.sig = False
        self.dma_t = None
        self.dval = 0
        self.sigcount = 0
        self.raw = ()


ENGS = ("pe", "act", "dve", "pool", "sp")


class Builder:
    def __init__(self, nc, same_engine_sync=False):
        self.nc = nc
        self.prog = {e: [] for e in ENGS}
        self.same_engine_sync = same_engine_sync
        self.dma_tiles = []
        self.arena_off = 0
        self.arena = None
        self.arena_size = 0

    def init_arena(self, nbytes):
        self.arena = self.nc.alloc_sbuf_tensor("arena", [128, nbytes], U8)
        self.arena_size = nbytes

    def alloc(self, name, shape, dt):
        esz = 4 if dt == F32 else 2
        n = int(np.prod(shape[1:]))
        nb = n * esz
        off = (self.arena_off + 63) // 64 * 64
        assert off + nb <= self.arena_size, f"arena overflow at {name}: {off + nb}"
        self.arena_off = off + nb
        ap = self.arena[:, off:off + nb].bitcast(dt)
        if len(shape) == 3:
            ap = ap.rearrange("p (a b) -> p a b", b=shape[2])
        elif len(shape) == 4:
            ap = ap.rearrange("p (a b c) -> p a b c", b=shape[2], c=shape[3])
        if shape[0] != 128:
            ap = ap[0:shape[0]]
        return Buf(ap, name)

    def mark(self):
        return self.arena_off

    def release(self, m):
        self.arena_off = m

    def _record(self, o, r, w, pw):
        deps = {}
        raw = set()

        def add(d):
            if d is not o:
                deps[id(d)] = d
        for b in r:
            for d in b.t.w.values():
                add(d)
                raw.add(id(d))
        for b in w:
            for d in b.t.w.values():
                add(d)
            for d in b.t.r.values():
                add(d)
        for b in pw:
            for d in b.t.w.values():
                add(d)
            for d in b.t.r.values():
                add(d)
        o.deps = list(deps.values())
        o.raw = raw
        key = o.eng if o.dma_t is None else ("dma", id(o.dma_t))
        for b in r:
            b.t.r[key] = o
        for b in w:
            b.t.w = {key: o}
            b.t.r = {}
        for b in pw:
            b.t.w[key] = o
        o.idx = len(self.prog[o.eng])
        self.prog[o.eng].append(o)
        return o

    def cop(self, eng, fn, r=(), w=(), pw=()):
        return self._record(Op(eng, fn), r, w, pw)

    def dma(self, q, out, in_, sb, r=(), w=(), pw=()):
        o = Op(q, lambda e: e.dma_start(out=out, in_=in_))
        t = sb.t
        if t.sem is None:
            t.sem = True
            self.dma_tiles.append(t)
        t.dcount += 1
        o.dma_t = t
        o.dval = 16 * t.dcount
        return self._record(o, r, w, pw)

    def barrier(self):
        lasts = [self.prog[e][-1] for e in ENGS if self.prog[e]]
        dl = []
        for t in self.dma_tiles:
            if t.dcount:
                d = Op("sp", None)
                d.dma_t = t
                d.dval = 16 * t.dcount
                dl.append(d)
        for e in ENGS:
            o = Op(e, None)
            o.deps = [x for x in lasts if x.eng != e] + dl
            o.idx = len(self.prog[e])
            self.prog[e].append(o)

    def emit(self):
        nc = self.nc
        es = contextlib.ExitStack()
        engsem = {e: es.enter_context(nc.semaphore("s_" + e)) for e in ENGS}
        for i, t in enumerate(self.dma_tiles):
            t.sem = es.enter_context(nc.semaphore(f"d{i}"))
        for e in ENGS:
            for o in self.prog[e]:
                for d in o.deps:
                    if d.dma_t is None:
                        d.sig = True
        for e in ENGS:
            c = 0
            for o in self.prog[e]:
                if o.sig:
                    c += 1
                o.sigcount = c
        engobj = {"pe": nc.tensor, "act": nc.scalar, "dve": nc.vector, "pool": nc.gpsimd, "sp": nc.sync}
        nwaits = 0
        for e in ENGS:
            eo = engobj[e]
            waited = {}
            for o in self.prog[e]:
                need = {}
                for d in o.deps:
                    if d.dma_t is not None:
                        k, v = ("d", id(d.dma_t)), d.dval
                        sem = d.dma_t.sem
                    else:
                        if d.eng == e and (e == "pe" or e == "sp" or
                                           (not self.same_engine_sync and id(d) not in o.raw)):
                            continue
                        k, v = ("e", d.eng), d.sigcount
                        sem = engsem[d.eng]
                    if need.get(k, (0, None))[0] < v:
                        need[k] = (v, sem)
                for k, (v, sem) in need.items():
                    if waited.get(k, 0) >= v:
                        continue
                    eo.wait_ge(sem, v)
                    nwaits += 1
                    waited[k] = v
                if o.fn is None:
                    continue
                ins = o.fn(eo)
                if o.dma_t is not None:
                    ins.then_inc(o.dma_t.sem, 16)
                elif o.sig:
                    ins.then_inc(engsem[e], 1)
        for t in self.dma_tiles:
            if t.dcount:
                nc.gpsimd.wait_ge(t.sem, 16 * t.dcount)
        self.stats = {e: len(self.prog[e]) for e in ENGS}
        self.stats["waits"] = nwaits
        return es


def build_program(debug=False, stop_after=None):
    nc = bass.Bass("TRN2", target_bir_lowering=False)
    B = Builder(nc)
    okind = "ExternalOutput" if debug else "Internal"

    XT = nc.dram_tensor("xt", [3, D, TOKW], F32, kind="ExternalInput").ap()
    WIN = nc.dram_tensor("w_in", [D, INW], F32, kind="ExternalInput").ap()
    WA = nc.dram_tensor("w_a", [D, D], F32, kind="ExternalInput").ap()
    WB = nc.dram_tensor("w_b", [D, D], F32, kind="ExternalInput").ap()
    WO = nc.dram_tensor("w_o", [D, D], F32, kind="ExternalInput").ap()
    PAR = nc.dram_tensor("params", [128, NPAR], F32, kind="ExternalInput").ap()
    CF = nc.dram_tensor("cf", [128, NCF], F32, kind="ExternalInput").ap()
    CBF = nc.dram_tensor("cbf", [128, NCB], BF16, kind="ExternalInput").ap()
    MAD = nc.dram_tensor("ma", [128, 8, 512], BF16, kind="ExternalInput").ap()
    MMD = nc.dram_tensor("mm", [128, 2, 128], F32, kind="ExternalInput").ap()
    OUT = nc.dram_tensor("out", [D, OWN], F32, kind="ExternalOutput").ap()

    def scr(name, shape, dt=BF16):
        return Buf(nc.dram_tensor(name, shape, dt, kind=okind).ap(), name)
    KTA = scr("kta", [8, 128, S])
    KA = scr("ka", [NBLK, 128, 1024])
    VA = scr("va", [NBLK, 128, 1024])
    KTB = scr("ktb", [8, 128, S])
    VB = scr("vb", [NBLK, 128, 1024])
    QTA = scr("qta", [8, 128, OWN])
    OT = scr("ot", [8, 128, OWN])
    ZAT = scr("zat", [8, 128, OWN])
    QTB = scr("qtb", [8, 128, OWN])
    ZBT = scr("zbt", [8, 128, OWN])
    GAT = scr("gat", [8, 128, OWN])
    GBT = scr("gbt", [8, 128, OWN])
    HA = scr("ha", [8, 128, OWN])
    HB = scr("hb", [8, 128, OWN])
    GTD = scr("gtd", [128, NBLK, 8], F32) if debug else None

    B.init_arena(207 * 1024)
    PS = [Buf(nc.alloc_psum_tensor(f"ps{i}", [128, 512], F32)[:], f"ps{i}") for i in range(8)]
    psi = [0]

    def nps():
        p = PS[psi[0] % 8]
        psi[0] += 1
        return p

    par = B.alloc("par", [128, NPAR], F32)
    cf = B.alloc("cf", [128, NCF], F32)
    cb = B.alloc("cb", [128, NCB], BF16)
    GT = B.alloc("gt", [128, NBLK, 8], F32)
    B.dma("sp", par.ap, PAR, par, w=[par])
    B.dma("sp", cf.ap, CF, cf, w=[cf])
    B.dma("sp", cb.ap, CBF, cb, w=[cb])
    B.cop("dve", lambda e: e.tensor_scalar(out=par[:, P_GK:P_GK + 1], in0=par[:, P_GK:P_GK + 1],
                                           scalar1=float(128 ** -0.5), scalar2=None, op0=ALU.mult),
          r=[par], w=[par])
    ones_b = cb[:, CB_ONES:CB_ONES + 128]
    id_b = cb[:, CB_ID:CB_ID + 128]
    tri_b = cb[:, CB_TRI:CB_TRI + 128]
    rest_b = cb[:, CB_REST:CB_REST + 128]
    ones_f = cf[:, CF_ONES:CF_ONES + 128]
    tri_f = cf[:, CF_TRI:CF_TRI + 128]

    base_mark = B.mark()

    hn = B.alloc("hn", [128, 8, TOKW], BF16)
    hn_t = [Buf(None, f"hn{j}") for j in range(9)]
    st32 = [B.alloc(f"st32_{i}", [128, 8, 512], F32) for i in range(2)]
    wbf = [B.alloc(f"wbf{i}", [128, 8, 512], BF16) for i in range(3)]
    t1 = [B.alloc(f"t1_{i}", [128, 512], F32) for i in range(2)]
    rstd = [B.alloc(f"rstd{i}", [128, 512], F32) for i in range(2)]
    sq1 = [B.alloc(f"sq1_{i}", [128, 512], BF16) for i in range(2)]
    ost = [B.alloc(f"ost{i}", [128, OWN], BF16) for i in range(2)]
    osth = [Buf(ost[1].ap[:, 0:2048], "osth0"), Buf(ost[1].ap[:, 2048:4096], "osth1")]
    pext = B.alloc("pext", [128, 32, 132], F32)
    acc = B.alloc("acc", [128, 32, 128], F32)
    sqbs = []
    for i in range(4):
        q_ = Buf(acc.ap.rearrange("p a b -> p (a b)")[:, 1024 * i:1024 * (i + 1)].bitcast(BF16).rearrange(
            "p (a b) -> p a b", b=256), f"sqb{i}")
        sqbs.append(q_)
    st4 = []
    for i in range(4):
        st4.append(Buf(st32[i // 2].ap[:, :, (i % 2) * 256:(i % 2) * 256 + 256], f"st4_{i}"))
    ktm = B.alloc("ktm", [128, 32, 128], BF16)
    vst = [B.alloc(f"vst{i}", [128, 4, 512], BF16) for i in range(2)]
    wg32 = B.alloc("wg32", [128, 8, 8], F32)
    wgb = B.alloc("wgb", [128, 8, 8], BF16)
    gtmp = B.alloc("gtmp", [128, 32, 4], F32)
    cnt = {"st": 0, "w": 0, "wf": 0, "ost": 0, "vst": 0, "r": 0}

    def load_weights(col0, ncols=512, wslot=None):
        if wslot == "c":
            k = 0
        elif wslot == "f":
            k = 1 + cnt["wf"] % 2
            cnt["wf"] += 1
        else:
            k = cnt["w"] % 3
            cnt["w"] += 1
        wb_ = wbf[k]
        B.dma("pool", wb_.ap[:, :, 0:ncols], WIN[:, col0:col0 + ncols].rearrange("(c p) n -> p c n", p=128),
              wb_, w=[wb_])
        return wb_

    def normalize(P):
        info = {}
        tiles = [(c0, 256) for c0 in range(0, OWN, 256)] + [(OWN, HALO)]

        def stage1(j):
            c0, W = tiles[j]
            k = cnt["st"] % 4
            cnt["st"] += 1
            st, sq = st4[k], sqbs[k]
            B.dma("sp", st.ap[:, :, 0:W], XT[P, :, c0:c0 + W].rearrange("(c p) t -> p c t", p=128),
                  st, w=[st])
            B.cop("act", lambda e: e.activation(out=sq.ap[:, :, 0:W], in_=st.ap[:, :, 0:W], func=AF.Square),
                  r=[st], w=[sq] + ([acc] if j < 4 else []))
            ps = nps()
            for c in range(8):
                B.cop("pe", lambda e, c=c: e.matmul(ps.ap[:, 0:W], lhsT=ones_b, rhs=sq.ap[:, c, 0:W],
                                                    start=(c == 0), stop=(c == 7)),
                      r=[sq, cb], w=[ps] if c == 0 else (), pw=[ps] if c else ())
            info[j] = (st, ps, c0, W)

        def stage2(j):
            st, ps, c0, W = info[j]
            k = cnt["r"] % 2
            cnt["r"] += 1
            tt, rs = t1[k], rstd[k]
            ht = hn_t[min(c0 // 512, 8)]
            first = (c0 % 512 == 0)
            B.cop("act", lambda e: e.activation(out=tt.ap[:, 0:W], in_=ps.ap[:, 0:W], func=AF.Ln,
                                                scale=1.0 / D, bias=EPS), r=[ps], w=[tt])
            B.cop("act", lambda e: e.activation(out=rs.ap[:, 0:W], in_=tt.ap[:, 0:W], func=AF.Exp,
                                                scale=-0.5), r=[tt], w=[rs])
            for c in range(8):
                B.cop("dve", lambda e, c=c: e.scalar_tensor_tensor(
                    out=hn.ap[:, c, c0:c0 + W], in0=st.ap[:, c, 0:W], scalar=par[:, P_GN + c:P_GN + c + 1],
                    in1=rs.ap[:, 0:W], op0=ALU.mult, op1=ALU.mult),
                    r=[st, rs, par], w=[ht] if (c == 0 and first) else (), pw=() if (c == 0 and first) else [ht])
        nt = len(tiles)
        stage1(0)
        stage1(1)
        for j in range(nt):
            if j + 2 < nt:
                stage1(j + 2)
            stage2(j)

    def fm_group(P, col0, typ, cc0, dest, gain_col=None, wslot=None):
        wb_ = load_weights(col0, wslot=wslot)
        tok0 = P * OWN if P < 2 else 0
        isconv = typ in ("convq", "convk")
        for nci in range(4):
            cc = cc0 + nci
            o_ = ost[0]
            ntl = 9 if isconv else 8
            for j in range(ntl):
                c0 = j * 512
                W = 512 if j < 8 else HALO
                if not isconv:
                    if j % 4 == 0:
                        o_ = osth[cnt["ost"] % 2]
                        cnt["ost"] += 1
                oc0 = (j % 4) * 512
                ps = nps()
                for c in range(8):
                    B.cop("pe", lambda e, ps=ps, c=c, W=W, c0=c0, nci=nci: e.matmul(
                        ps.ap[:, 0:W], lhsT=wb_.ap[:, c, nci * 128:(nci + 1) * 128], rhs=hn.ap[:, c, c0:c0 + W],
                        start=(c == 0), stop=(c == 7)),
                        r=[wb_, hn_t[j]], w=[ps] if c == 0 else (), pw=[ps] if c else ())
                if typ == "sig" or typ == "silu":
                    fn = AF.Sigmoid if typ == "sig" else AF.Silu
                    B.cop("act", lambda e, ps=ps, c0=oc0, fn=fn, o_=o_: e.activation(out=o_.ap[:, c0:c0 + 512], in_=ps.ap,
                                                                                   func=fn),
                          r=[ps], w=[o_] if j % 4 == 0 else (), pw=[o_] if j % 4 else ())
                elif typ == "qkn":
                    k = cnt["r"] % 2
                    cnt["r"] += 1
                    s1, tt, rs = sq1[k], t1[k], rstd[k]
                    B.cop("act", lambda e, ps=ps, s1=s1: e.activation(out=s1.ap, in_=ps.ap, func=AF.Square),
                          r=[ps], w=[s1])
                    ps2 = nps()
                    B.cop("pe", lambda e, ps2=ps2, s1=s1: e.matmul(ps2.ap, lhsT=ones_b, rhs=s1.ap, start=True, stop=True),
                          r=[s1, cb], w=[ps2])
                    B.cop("act", lambda e, ps2=ps2, tt=tt: e.activation(out=tt.ap, in_=ps2.ap, func=AF.Ln,
                                                                       scale=1.0 / 128, bias=EPS), r=[ps2], w=[tt])
                    B.cop("act", lambda e, tt=tt, rs=rs: e.activation(out=rs.ap, in_=tt.ap, func=AF.Exp, scale=-0.5),
                          r=[tt], w=[rs])
                    B.cop("dve", lambda e, ps=ps, rs=rs, c0=oc0, o_=o_: e.scalar_tensor_tensor(
                        out=o_.ap[:, c0:c0 + 512], in0=ps.ap, scalar=par[:, gain_col:gain_col + 1], in1=rs.ap,
                        op0=ALU.mult, op1=ALU.mult),
                        r=[ps, rs, par], w=[o_] if j % 4 == 0 else (), pw=[o_] if j % 4 else ())
                else:
                    if j < 8:
                        B.cop("act", lambda e, ps=ps, j=j: e.activation(
                            out=pext.ap[:, 4 * j:4 * j + 4, 4:132],
                            in_=ps.ap.rearrange("p (a b) -> p a b", b=128), func=AF.Copy),
                            r=[ps], w=[pext] if j == 0 else (), pw=[pext] if j else ())
                    else:
                        B.cop("act", lambda e, ps=ps: e.activation(
                            out=pext.ap[:, :, 0:4], in_=ps.ap[:, 0:128].rearrange("p (a b) -> p a b", b=4),
                            func=AF.Copy), r=[ps], pw=[pext])
                if (not isconv) and j % 4 == 3:
                    hb0 = tok0 + (j // 4) * 2048
                    B.dma("sp", dest.ap[cc, :, hb0:hb0 + 2048], o_.ap, o_, r=[o_], pw=[dest])
            if typ in ("convq", "convk"):
                pc = P_CW + (cc if typ == "convq" else 8 + cc) * 4
                pb = P_CB + (cc if typ == "convq" else 8 + cc)
                B.cop("dve", lambda e, pc=pc, pb=pb: e.tensor_scalar(
                    out=acc.ap, in0=pext.ap[:, :, 1:129], scalar1=par[:, pc:pc + 1], scalar2=par[:, pb:pb + 1],
                    op0=ALU.mult, op1=ALU.add), r=[pext, par], w=[acc])
                for k in range(1, 4):
                    B.cop("dve", lambda e, pc=pc, k=k: e.scalar_tensor_tensor(
                        out=acc.ap, in0=pext.ap[:, :, k + 1:k + 129], scalar=par[:, pc + k:pc + k + 1], in1=acc.ap,
                        op0=ALU.mult, op1=ALU.add), r=[pext, par, acc], w=[acc])
                yield
                o3 = o_.ap.rearrange("p (a b) -> p a b", b=128)
                if typ == "convk":
                    B.cop("act", lambda e, o3=o3: e.activation(out=o3, in_=acc.ap, func=AF.Silu), r=[acc], w=[o_])
                else:
                    B.cop("act", lambda e: e.activation(out=acc.ap, in_=acc.ap, func=AF.Silu), r=[acc], w=[acc])
                    B.cop("act", lambda e, o3=o3: e.activation(out=o3, in_=acc.ap, func=AF.Copy, scale=1.0 / 16.0),
                          r=[acc], w=[o_])
            if typ == "convk":
                for g in range(8):
                    ps = nps()
                    psb = ps.ap.bitcast(BF16)
                    for i in range(4):
                        blk = g * 4 + i
                        B.cop("pe", lambda e, psb=psb, i=i, blk=blk, o_=o_: e.transpose(
                            out=psb[:, i * 128:(i + 1) * 128], in_=o_.ap[:, blk * 128:(blk + 1) * 128], identity=id_b),
                            r=[o_, cb], w=[ps] if i == 0 else (), pw=[ps] if i else ())
                    eng = "dve" if g % 2 == 0 else "act"
                    dst = ktm.ap[:, g * 4:(g + 1) * 4, :]
                    src = psb[:, 0:512].rearrange("p (a b) -> p a b", b=128)
                    if eng == "dve":
                        B.cop("dve", lambda e, dst=dst, src=src: e.tensor_copy(out=dst, in_=src),
                              r=[ps], w=[ktm] if g == 0 else (), pw=[ktm] if g else ())
                    else:
                        B.cop("act", lambda e, dst=dst, src=src: e.activation(out=dst, in_=src, func=AF.Copy),
                              r=[ps], pw=[ktm])
                B.dma("sp", KA.ap[P * 32:(P + 1) * 32, :, cc * 128:(cc + 1) * 128].rearrange("b s c -> s b c"),
                      ktm.ap, ktm, r=[ktm], pw=[KA])
            if isconv:
                B.dma("sp", dest.ap[cc, :, tok0:tok0 + OWN], o_.ap, o_, r=[o_], pw=[dest])
            yield

    def tm_group(P, col0, dest, half, wslot=None, act_only=False):
        wb_ = load_weights(col0, wslot=wslot)
        for g in range(8):
            v_ = vst[cnt["vst"] % 2]
            cnt["vst"] += 1
            for i in range(4):
                blk = g * 4 + i
                ps = nps()
                for c in range(8):
                    B.cop("pe", lambda e, ps=ps, c=c, blk=blk: e.matmul(
                        ps.ap, lhsT=hn.ap[:, c, blk * 128:(blk + 1) * 128], rhs=wb_.ap[:, c, :],
                        start=(c == 0), stop=(c == 7)),
                        r=[wb_, hn_t[blk // 4]], w=[ps] if c == 0 else (), pw=[ps] if c else ())
                if i % 2 == 0 and not act_only:
                    B.cop("dve", lambda e, ps=ps, i=i, v_=v_: e.tensor_copy(out=v_.ap[:, i, :], in_=ps.ap),
                          r=[ps], w=[v_] if i == 0 else (), pw=[v_] if i else ())
                else:
                    B.cop("act", lambda e, ps=ps, i=i, v_=v_: e.activation(out=v_.ap[:, i, :], in_=ps.ap, func=AF.Copy),
                          r=[ps], w=[v_] if i == 0 else (), pw=[v_] if i else ())
            B.dma("sp", dest.ap[P * 32 + g * 4:P * 32 + g * 4 + 4, :, half * 512:(half + 1) * 512].rearrange(
                "b s c -> s b c"), v_.ap, v_, r=[v_], pw=[dest])
            yield

    def gates(P):
        B.dma("pool", wgb.ap, WIN[:, C_IG:C_IG + 8].rearrange("(c p) n -> p c n", p=128), wgb, w=[wgb])
        ps = nps()
        first = True
        for blk in range(32):
            for c in range(8):
                B.cop("pe", lambda e, c=c, blk=blk, ps=ps: e.matmul(
                    ps.ap[:, blk * 8:(blk + 1) * 8], lhsT=hn.ap[:, c, blk * 128:(blk + 1) * 128], rhs=wgb.ap[:, c, :],
                    start=(c == 0), stop=(c == 7)),
                    r=[wgb, hn_t[blk // 4]], w=[ps] if first else (), pw=() if first else [ps])
                first = False
        p3 = ps.ap[:, 0:256].rearrange("p (a b) -> p a b", b=8)
        gsl = GT.ap[:, P * 32:(P + 1) * 32, :]
        B.cop("dve", lambda e: e.tensor_tensor(
            out=gsl[:, :, 0:4], in0=p3[:, :, 0:4],
            in1=par[:, P_IB:P_IB + 4].unsqueeze(1).to_broadcast([128, 32, 4]), op=ALU.add),
            r=[ps, par], pw=[GT])
        B.cop("dve", lambda e: e.tensor_tensor(
            out=gtmp.ap, in0=p3[:, :, 4:8],
            in1=par[:, P_FB:P_FB + 4].unsqueeze(1).to_broadcast([128, 32, 4]), op=ALU.add),
            r=[ps, par], w=[gtmp])
        B.cop("act", lambda e: e.activation(out=gtmp.ap, in_=gtmp.ap, func=AF.Exp, scale=-1.0), r=[gtmp], w=[gtmp])
        B.cop("act", lambda e: e.activation(out=gtmp.ap, in_=gtmp.ap, func=AF.Ln, bias=1.0), r=[gtmp], w=[gtmp])
        B.cop("dve", lambda e: e.tensor_scalar(out=gsl[:, :, 4:8], in0=gtmp.ap, scalar1=-1.0, scalar2=None,
                                               op0=ALU.mult), r=[gtmp], pw=[GT])

    import itertools

    def run(g):
        for _ in g:
            pass

    def interleave(conv, fill, nfill, two_part=False):
        n = 0
        for _ in conv:
            if (not two_part) or n % 2 == 0:
                for _k in range(nfill):
                    next(fill, None)
            n += 1
        run(fill)

    for P in range(3):
        normalize(P)
        if P < 2:
            conv = itertools.chain(fm_group(P, C_KA, "convk", 0, KTA, wslot="c"),
                                   fm_group(P, C_KA + 512, "convk", 4, KTA, wslot="c"))
            fill = itertools.chain(tm_group(P, C_VA, VA, 0, wslot="f", act_only=True),
                                   tm_group(P, C_VA + 512, VA, 1, wslot="f", act_only=True),
                                   tm_group(P, C_VB, VB, 0, wslot="f", act_only=True),
                                   tm_group(P, C_VB + 512, VB, 1, wslot="f", act_only=True))
            interleave(conv, fill, 4, two_part=True)
            gates(P)
            run(fm_group(P, C_KB, "qkn", 0, KTB, gain_col=P_GK))
            run(fm_group(P, C_KB + 512, "qkn", 4, KTB, gain_col=P_GK))
        else:
            conv = itertools.chain(fm_group(P, C_QA, "convq", 0, QTA, wslot="c"),
                                   fm_group(P, C_QA + 512, "convq", 4, QTA, wslot="c"))
            fl = []
            for (c, typ, dst) in ((C_OA, "sig", OT), (C_ZA, "silu", ZAT), (C_ZB, "silu", ZBT), (C_GA, "sig", GAT),
                                  (C_GB, "sig", GBT)):
                fl.append((c, typ, 0, dst))
                fl.append((c + 512, typ, 4, dst))
            fill = itertools.chain(*[fm_group(P, c, typ, cc0, dst, wslot="f") for (c, typ, cc0, dst) in fl[:6]])
            interleave(conv, fill, 3, two_part=True)
            for n, (c, typ, cc0, dst) in enumerate(fl[6:]):
                run(fm_group(P, c, typ, cc0, dst))
            run(fm_group(P, C_QB, "qkn", 0, QTB, gain_col=P_GQ))
            run(fm_group(P, C_QB + 512, "qkn", 4, QTB, gain_col=P_GQ))
    if debug:
        B.dma("pool", GTD.ap, GT.ap, GT, r=[GT], pw=[GTD])

    B.barrier()
    B.release(base_mark)

    if stop_after != "B":
        mab = B.alloc("mab", [128, 8, 512], BF16)
        B.dma("sp", mab.ap, MAD, mab, w=[mab])
        kt = [B.alloc(f"kt{i}", [128, S], BF16) for i in range(2)]
        vv = [B.alloc(f"vv{i}", [128, NBLK, 128], BF16) for i in range(2)]
        qt = [B.alloc(f"qt{i}", [128, OWN], BF16) for i in range(2)]
        zb = [B.alloc(f"zb{i}", [128, OWN], BF16) for i in range(2)]
        hbst = [B.alloc(f"hbst{i}", [128, OWN], BF16) for i in range(2)]
        NW = 3
        eb = [[B.alloc(f"e{s}{i}", [128, 512], BF16) for i in range(NW)] for s in range(2)]
        lb = [[B.alloc(f"l{s}{i}", [128, 512], BF16) for i in range(NW)] for s in range(2)]
        e2 = [[B.alloc(f"x{s}{i}", [128, 512], BF16) for i in range(NW)] for s in range(2)]
        wt = [[B.alloc(f"w{s}{i}", [128, 512], BF16) for i in range(NW)] for s in range(2)]
        def attn_loads(h):
            k = h % 2
            kt_, vv_, qt_, zb_ = kt[k], vv[k], qt[k], zb[k]
            B.dma("sp", kt_.ap, KTB.ap[h], kt_, r=[KTB], w=[kt_])
            B.dma("sp", qt_.ap, QTB.ap[h], qt_, r=[QTB], w=[qt_])
            B.dma("sp", vv_.ap, VB.ap[:, :, h * 128:(h + 1) * 128].rearrange("b s c -> s b c"), vv_, r=[VB], w=[vv_])
            B.dma("sp", zb_.ap, ZBT.ap[h], zb_, r=[ZBT], w=[zb_])

        def attn_head(h):
            k = h % 2
            kt_, vv_, qt_, zb_, hb_ = kt[k], vv[k], qt[k], zb[k], hbst[k]
            if h + 1 < 8:
                attn_loads(h + 1)
            streams = []
            for s, Is in enumerate(([7, 0, 5, 2], [6, 1, 4, 3])):
                units = []
                for I in Is:
                    n = 8 * I + 8
                    for u in range(n):
                        units.append((I, n - 1 - u, u == 0, u == n - 1))
                streams.append(units)
            nun = len(streams[0])
            assert nun == len(streams[1])
            first_hb = [True]

            def cm(I, kb):
                return 128 * ((kb - 8 * I) // 2) if kb >= 8 * I else 0

            def qk(s, ui):
                I, kb, fst, lst = streams[s][ui]
                c = cm(I, kb)
                z = PS[2 * s + ui % 2]
                B.cop("pe", lambda e: e.matmul(z.ap[:, c:512], lhsT=kt_.ap[:, kb * 128:(kb + 1) * 128],
                                               rhs=qt_.ap[:, I * 512 + c:(I + 1) * 512], start=True, stop=True),
                      r=[kt_, qt_], w=[z])

            def st_exp(s, ui):
                I, kb, fst, lst = streams[s][ui]
                c = cm(I, kb)
                z = PS[2 * s + ui % 2]
                e_ = eb[s][ui % NW]
                B.cop("act", lambda e: e.activation(out=e_.ap[:, c:512], in_=z.ap[:, c:512], func=AF.Exp), r=[z], w=[e_])
                if kb >= 8 * I:
                    m = kb - 8 * I
                    B.cop("dve", lambda e: e.tensor_tensor(out=e_.ap[:, c:512], in0=e_.ap[:, c:512],
                                                           in1=mab.ap[:, m, c:512], op=ALU.mult),
                          r=[e_, mab], w=[e_])

            def st_ln(s, ui):
                I, kb, fst, lst = streams[s][ui]
                c = cm(I, kb)
                e_ = eb[s][ui % NW]
                l_ = lb[s][ui % NW]
                B.cop("act", lambda e: e.activation(out=l_.ap[:, c:512], in_=e_.ap[:, c:512], func=AF.Ln, bias=1.0),
                      r=[e_], w=[l_])

            def st_tri(s, ui):
                I, kb, fst, lst = streams[s][ui]
                c = cm(I, kb)
                l_ = lb[s][ui % NW]
                sp_ = PS[4 + s]
                B.cop("pe", lambda e: e.matmul(sp_.ap[:, c:512], lhsT=tri_b, rhs=l_.ap[:, c:512], start=fst, stop=True,
                                               skip_group_check=not fst), r=[l_, cb], w=[sp_])

            def st_exp2(s, ui):
                I, kb, fst, lst = streams[s][ui]
                c = cm(I, kb)
                sp_ = PS[4 + s]
                x_ = e2[s][ui % NW]
                B.cop("act", lambda e: e.activation(out=x_.ap[:, c:512], in_=sp_.ap[:, c:512], func=AF.Exp),
                      r=[sp_], w=[x_])

            def st_rest(s, ui):
                I, kb, fst, lst = streams[s][ui]
                if lst:
                    return
                c = cm(I, kb)
                l_ = lb[s][ui % NW]
                sp_ = PS[4 + s]
                B.cop("pe", lambda e: e.matmul(sp_.ap[:, c:512], lhsT=rest_b, rhs=l_.ap[:, c:512], start=False, stop=True,
                                               skip_group_check=True), r=[l_, cb], w=[sp_])

            def st_mul(s, ui):
                I, kb, fst, lst = streams[s][ui]
                c = cm(I, kb)
                e_ = eb[s][ui % NW]
                x_ = e2[s][ui % NW]
                w_ = wt[s][ui % NW]
                B.cop("dve", lambda e: e.tensor_tensor(out=w_.ap[:, c:512], in0=e_.ap[:, c:512], in1=x_.ap[:, c:512],
                                                       op=ALU.mult), r=[e_, x_], w=[w_])

            def st_pv(s, ui):
                I, kb, fst, lst = streams[s][ui]
                c = cm(I, kb)
                w_ = wt[s][ui % NW]
                op_ = PS[6 + s]
                B.cop("pe", lambda e: e.matmul(op_.ap[:, c:512], lhsT=vv_.ap[:, kb, :], rhs=w_.ap[:, c:512], start=fst,
                                               stop=lst, skip_group_check=True),
                      r=[vv_, w_], w=[op_] if fst else (), pw=() if fst else [op_])
                if lst:
                    B.cop("dve", lambda e: e.tensor_tensor(out=hb_.ap[:, I * 512:(I + 1) * 512], in0=op_.ap,
                                                           in1=zb_.ap[:, I * 512:(I + 1) * 512], op=ALU.mult),
                          r=[op_, zb_], w=[hb_] if first_hb[0] else (), pw=() if first_hb[0] else [hb_])
                    first_hb[0] = False

            for s in range(2):
                qk(s, 0)
            for s in range(2):
                qk(s, 1)
            for s in range(2):
                st_exp(s, 0)
            for ui in range(nun):
                for s in range(2):
                    st_ln(s, ui)
                for s in range(2):
                    st_tri(s, ui)
                if ui + 1 < nun:
                    for s in range(2):
                        st_exp(s, ui + 1)
                if ui + 2 < nun:
                    for s in range(2):
                        qk(s, ui + 2)
                for s in range(2):
                    st_exp2(s, ui)
                for s in range(2):
                    st_rest(s, ui)
                for s in range(2):
                    st_mul(s, ui)
                for s in range(2):
                    st_pv(s, ui)
            B.dma("pool", HB.ap[h], hb_.ap, hb_, r=[hb_], pw=[HB])
        attn_loads(0)
        for h in range(8):
            attn_head(h)
        B.barrier()
        B.release(base_mark)

    if stop_after not in ("B", "C"):
        ewts = []
        for nm, Wd in (("wa", WA), ("wb", WB), ("wo", WO)):
            wt_ = B.alloc(nm, [128, 8, 1024], BF16)
            for hf in range(2):
                B.dma("pool", wt_.ap[:, :, hf * 512:(hf + 1) * 512],
                      Wd[:, hf * 512:(hf + 1) * 512].rearrange("(c p) n -> p c n", p=128), wt_,
                      w=[wt_] if hf == 0 else (), pw=[wt_] if hf else ())
            ewts.append(wt_)
        e_mark = B.mark()
        mmf = B.alloc("mmf", [128, 2, 128], F32)
        mmb = B.alloc("mmb", [128, 2, 128], BF16)
        B.dma("sp", mmf.ap, MMD, mmf, w=[mmf])
        B.cop("dve", lambda e: e.tensor_copy(out=mmb.ap, in_=mmf.ap), r=[mmf], w=[mmb])
        hastS = [B.alloc(f"hast{i}", [128, 8, 128], BF16) for i in range(2)]
        C32 = B.alloc("c32", [128, 2, 4, 257], F32)
        Cbf = [B.alloc(f"cbf{i}", [128, 2, 4, 257], BF16) for i in range(2)]
        nbb = [B.alloc(f"nbb{i}", [128, 2, 4, 128], BF16) for i in range(2)]
        ktaL = [B.alloc(f"ktaL{i}", [128, 8, 256], BF16) for i in range(2)]
        kaL = [B.alloc(f"kaL{i}", [128, 2, 1024], BF16) for i in range(2)]
        vaL = [B.alloc(f"vaL{i}", [128, 8, 257], BF16) for i in range(2)]
        qtaL = [B.alloc(f"qtaL{i}", [128, 8, 128], BF16) for i in range(2)]
        otL = [B.alloc(f"otL{i}", [128, 8, 128], BF16) for i in range(2)]
        zaL = [B.alloc(f"zaL{i}", [128, 8, 128], BF16) for i in range(2)]
        lfbS = [B.alloc(f"lfb{i}", [128, 8, 128], F32) for i in range(2)]
        cbDS = [B.alloc(f"cbd{i}", [128, 8], F32) for i in range(2)]
        waDS = [B.alloc(f"wad{i}", [128, 8], F32) for i in range(2)]
        waBS = [B.alloc(f"wab{i}", [128, 8], BF16) for i in range(2)]
        decS = [B.alloc(f"dec{i}", [128, 4], F32) for i in range(2)]
        vwS = [B.alloc(f"vw{i}", [128, 8, 257], BF16) for i in range(2)]
        NS = 4
        ebD = [B.alloc(f"ebd{i}", [128, 128], F32) for i in range(NS)]
        qs = [B.alloc(f"qs{i}", [128, 2, 128], BF16) for i in range(NS)]
        dt_ = [B.alloc(f"dt{i}", [128, 2, 128], F32) for i in range(NS)]
        ptm = [B.alloc(f"ptm{i}", [128, 2, 128], BF16) for i in range(NS)]
        pt = [B.alloc(f"pt{i}", [128, 2, 128], BF16) for i in range(NS)]
        rr = [B.alloc(f"rr{i}", [128, 128], F32) for i in range(NS)]
        u1 = [B.alloc(f"u1{i}", [128, 2, 128], F32) for i in range(NS)]
        uu = [B.alloc(f"uu{i}", [128, 2, 128], F32) for i in range(NS)]
        usq = [B.alloc(f"usq{i}", [128, 2, 128], BF16) for i in range(NS)]
        tD = [B.alloc(f"td{i}", [128, 128], F32) for i in range(NS)]
        rsD = [B.alloc(f"rsd{i}", [128, 128], F32) for i in range(NS)]
        g1 = [B.alloc(f"g1{i}", [128, 2, 128], F32) for i in range(NS)]
        nsb = [B.alloc(f"nsb{i}", [128, 2], F32) for i in range(NS)]
        B.cop("dve", lambda e: e.memset(C32.ap, 0.0), w=[C32])
        B.cop("pool", lambda e: e.memset(Cbf[0].ap, 0.0), w=[Cbf[0]])
        B.cop("pool", lambda e: e.memset(nbb[0].ap, 0.0), w=[nbb[0]])
        for i in range(2):
            B.cop("pool", lambda e, i=i: e.memset(vaL[i].ap, 1.0), w=[vaL[i]])
        def sc_loads(i):
            k = i % 2
            kta_, ka_, va_, qta_, ot_, za_ = ktaL[k], kaL[k], vaL[k], qtaL[k], otL[k], zaL[k]
            B.dma("sp", kta_.ap, KTA.ap[:, :, i * 256:(i + 1) * 256].rearrange("c p s -> p c s"), kta_, r=[KTA], w=[kta_])
            B.dma("sp", ka_.ap, KA.ap[2 * i:2 * i + 2].rearrange("b s c -> s b c"), ka_, r=[KA], w=[ka_])
            for sc in range(2):
                B.dma("sp", va_.ap[:, sc * 4:(sc + 1) * 4, 0:256],
                      VA.ap[2 * i + sc].rearrange("s (h v) -> s h v", v=256), va_, r=[VA], pw=[va_])
            B.dma("sp", qta_.ap, QTA.ap[:, :, i * 128:(i + 1) * 128].rearrange("c p s -> p c s"), qta_, r=[QTA], w=[qta_])
            B.dma("sp", ot_.ap, OT.ap[:, :, i * 128:(i + 1) * 128].rearrange("c p s -> p c s"), ot_, r=[OT], w=[ot_])
            B.dma("sp", za_.ap, ZAT.ap[:, :, i * 128:(i + 1) * 128].rearrange("c p s -> p c s"), za_, r=[ZAT], w=[za_])

        def superchunk(i):
            k = i % 2
            kta_, ka_, va_, qta_, ot_, za_ = ktaL[k], kaL[k], vaL[k], qtaL[k], otL[k], zaL[k]
            if i + 1 < 32:
                sc_loads(i + 1)
            cprev, cnext = Cbf[i % 2], Cbf[(i + 1) % 2]
            nprev, nnext = nbb[i % 2], nbb[(i + 1) % 2]
            lfb_, cbD_, waD_, waB_, dec_, vw_ = lfbS[k], cbDS[k], waDS[k], waBS[k], decS[k], vwS[k]
            hast_ = hastS[k]
            lf_in = GT.ap[:, 2 * i:2 * i + 2, 4:8]
            ig_in = GT.ap[:, 2 * i:2 * i + 2, 0:4]
            B.cop("dve", lambda e: e.tensor_copy(
                out=lfb_.ap.rearrange("p (a b) s -> p a b s", b=4),
                in_=lf_in.unsqueeze(3).to_broadcast([128, 2, 4, 128])), r=[GT], w=[lfb_])
            psB = PS[1]
            psS = psB.ap[:, 400:412]
            B.cop("pe", lambda e: e.matmul(psS[:, 0:4], lhsT=tri_f, rhs=GT.ap[:, 2 * i, 4:8],
                                           start=True, stop=True), r=[GT, cf], w=[psB])
            B.cop("pe", lambda e: e.matmul(psS[:, 4:8], lhsT=ones_f, rhs=GT.ap[:, 2 * i, 4:8],
                                           start=True, stop=False), r=[GT, cf], pw=[psB])
            B.cop("pe", lambda e: e.matmul(psS[:, 4:8], lhsT=tri_f, rhs=GT.ap[:, 2 * i + 1, 4:8],
                                           start=False, stop=True), r=[GT, cf], pw=[psB])
            B.cop("pe", lambda e: e.matmul(psS[:, 8:12], lhsT=ones_f, rhs=GT.ap[:, 2 * i, 4:8],
                                           start=True, stop=False), r=[GT, cf], pw=[psB])
            B.cop("pe", lambda e: e.matmul(psS[:, 8:12], lhsT=ones_f, rhs=GT.ap[:, 2 * i + 1, 4:8],
                                           start=False, stop=True), r=[GT, cf], pw=[psB])
            B.cop("dve", lambda e: e.scalar_tensor_tensor(
                out=cbD_.ap.rearrange("p (a b) -> p a b", b=4), in0=psS[:, 0:8].rearrange("p (a b) -> p a b", b=4),
                scalar=-1.0, in1=ig_in, op0=ALU.mult, op1=ALU.add), r=[psB, GT], w=[cbD_])
            B.cop("dve", lambda e: e.tensor_tensor(
                out=waD_.ap.rearrange("p (a b) -> p a b", b=4), in0=cbD_.ap.rearrange("p (a b) -> p a b", b=4),
                in1=psS[:, 8:12].unsqueeze(1).to_broadcast([128, 2, 4]), op=ALU.add), r=[psB, cbD_], w=[waD_])
            B.cop("act", lambda e: e.activation(out=waD_.ap, in_=waD_.ap, func=AF.Exp), r=[waD_], w=[waD_])
            B.cop("act", lambda e: e.activation(out=dec_.ap, in_=psS[:, 8:12], func=AF.Exp), r=[psB], w=[dec_])
            B.cop("dve", lambda e: e.tensor_copy(out=waB_.ap, in_=waD_.ap), r=[waD_], w=[waB_])
            B.cop("pool", lambda e: e.tensor_tensor(
                out=vw_.ap, in0=va_.ap, in1=waD_.ap.unsqueeze(2).to_broadcast([128, 8, 257]), op=ALU.mult),
                r=[va_, waD_], w=[vw_])

            def headpart(h):
                kk = h
                bX, bY = PS[2 * h], PS[2 * h + 1]
                eb_, qs_, dt1, ptm_, pt_ = ebD[kk], qs[kk], dt_[kk], ptm[kk], pt[kk]
                r_, u1_, u_, usq_, t_, rs_, g1_, nsb_ = rr[kk], u1[kk], uu[kk], usq[kk], tD[kk], rsD[kk], g1[kk], nsb[kk]
                for jc in range(2):
                    B.cop("pe", lambda e, jc=jc: e.matmul(
                        bX.ap[:, 256:384], lhsT=lfb_.ap[:, jc * 4 + h, :], rhs=mmf.ap[:, jc, :],
                        start=(jc == 0), stop=(jc == 1)), r=[lfb_, mmf], w=[bX] if jc == 0 else (), pw=[bX] if jc else ())
                for sc in range(2):
                    for kc in range(2):
                        B.cop("pe", lambda e, sc=sc, kc=kc: e.matmul(
                            bX.ap[:, sc * 128:(sc + 1) * 128], lhsT=kta_.ap[:, h * 2 + kc, sc * 128:(sc + 1) * 128],
                            rhs=qta_.ap[:, h * 2 + kc, :], start=(kc == 0), stop=(kc == 1)),
                            r=[kta_, qta_], pw=[bX])
                yield
                B.cop("act", lambda e: e.activation(out=eb_.ap, in_=bX.ap[:, 256:384], func=AF.Exp), r=[bX], w=[eb_])
                for sc in range(2):
                    B.cop("act", lambda e, sc=sc: e.activation(
                        out=dt1.ap[:, sc, :], in_=bX.ap[:, 256:384], func=AF.Exp,
                        bias=cbD_.ap[:, sc * 4 + h:sc * 4 + h + 1]), r=[bX, cbD_], w=[dt1] if sc == 0 else (),
                        pw=[dt1] if sc else ())
                B.cop("dve", lambda e: e.tensor_tensor(
                    out=qs_.ap, in0=qta_.ap[:, h * 2:h * 2 + 2, :],
                    in1=eb_.ap.unsqueeze(1).to_broadcast([128, 2, 128]), op=ALU.mult), r=[qta_, eb_], w=[qs_])
                yield
                B.cop("dve", lambda e: e.tensor_tensor(
                    out=ptm_.ap, in0=bX.ap[:, 0:256].rearrange("p (a b) -> p a b", b=128), in1=dt1.ap, op=ALU.mult),
                    r=[bX, dt1], w=[ptm_])
                B.cop("pool", lambda e: e.tensor_tensor(out=pt_.ap, in0=ptm_.ap, in1=mmb.ap, op=ALU.mult),
                      r=[ptm_, mmb], w=[pt_])
                yield
                first = True
                for vc in range(2):
                    for sc in range(2):
                        B.cop("pe", lambda e, vc=vc, sc=sc: e.matmul(
                            bY.ap[:, vc * 128:(vc + 1) * 128], lhsT=va_.ap[:, sc * 4 + h, vc * 128:(vc + 1) * 128],
                            rhs=pt_.ap[:, sc, :], start=(sc == 0), stop=False),
                            r=[va_, pt_], w=[bY] if first else (), pw=() if first else [bY])
                        first = False
                    for kc in range(2):
                        B.cop("pe", lambda e, vc=vc, kc=kc: e.matmul(
                            bY.ap[:, vc * 128:(vc + 1) * 128], lhsT=cprev.ap[:, kc, h, vc * 128:(vc + 1) * 128],
                            rhs=qs_.ap[:, kc, :], start=False, stop=(kc == 1)), r=[cprev, qs_], pw=[bY])
                for sc in range(2):
                    B.cop("pe", lambda e, sc=sc: e.matmul(
                        bX.ap[:, 384:512], lhsT=ones_b, rhs=pt_.ap[:, sc, :], start=(sc == 0), stop=False),
                        r=[pt_, cb], w=[bX] if sc == 0 else (), pw=[bX] if sc else ())
                for kc in range(2):
                    B.cop("pe", lambda e, kc=kc: e.matmul(
                        bX.ap[:, 384:512], lhsT=nprev.ap[:, kc, h, :], rhs=qs_.ap[:, kc, :], start=False,
                        stop=(kc == 1)), r=[nprev, qs_], pw=[bX])
                for kc in range(2):
                    for sc in range(2):
                        B.cop("pe", lambda e, kc=kc, sc=sc: e.matmul(
                            bY.ap[:, 384 + kc:385 + kc],
                            lhsT=ka_.ap[:, sc, h * 256 + kc * 128:h * 256 + (kc + 1) * 128],
                            rhs=waB_.ap[:, sc * 4 + h:sc * 4 + h + 1], start=(sc == 0), stop=(sc == 1)),
                            r=[ka_, waB_], pw=[bY])
                yield
                B.cop("dve", lambda e: e.tensor_scalar(out=r_.ap, in0=bX.ap[:, 384:512], scalar1=-1.0,
                                                       scalar2=None, op0=ALU.mult), r=[bX], w=[r_])
                B.cop("dve", lambda e: e.scalar_tensor_tensor(out=r_.ap, in0=r_.ap, scalar=1.0,
                                                              in1=bX.ap[:, 384:512], op0=ALU.max, op1=ALU.max),
                      r=[bX, r_], w=[r_])
                B.cop("dve", lambda e: e.tensor_tensor(
                    out=u1_.ap, in0=bY.ap[:, 0:256].rearrange("p (a b) -> p a b", b=128),
                    in1=ot_.ap[:, h * 2:h * 2 + 2, :], op=ALU.mult), r=[bY, ot_], w=[u1_])
                B.cop("act", lambda e: e.activation(out=r_.ap, in_=r_.ap, func=AF.Ln), r=[r_], w=[r_])
                B.cop("act", lambda e: e.activation(out=r_.ap, in_=r_.ap, func=AF.Exp, scale=-1.0), r=[r_], w=[r_])
                B.cop("act", lambda e: e.activation(out=nsb_.ap, in_=bY.ap[:, 384:386], func=AF.Copy), r=[bY], w=[nsb_])
                B.cop("dve", lambda e: e.scalar_tensor_tensor(
                    out=C32.ap[:, :, h, 256], in0=C32.ap[:, :, h, 256], scalar=dec_.ap[:, h:h + 1],
                    in1=nsb_.ap, op0=ALU.mult, op1=ALU.add), r=[nsb_, dec_, C32], w=[C32])
                B.cop("pool", lambda e: e.tensor_copy(
                    out=nnext.ap[:, :, h, :], in_=C32.ap[:, :, h, 256:257].to_broadcast([128, 2, 128])),
                    r=[C32], w=[nnext] if h == 0 else (), pw=[nnext] if h else ())
                yield
                B.cop("dve", lambda e: e.tensor_tensor(
                    out=u_.ap, in0=u1_.ap, in1=r_.ap.unsqueeze(1).to_broadcast([128, 2, 128]), op=ALU.mult),
                    r=[u1_, r_], w=[u_])
                B.cop("act", lambda e: e.activation(out=usq_.ap, in_=u_.ap, func=AF.Square), r=[u_], w=[usq_])
                yield
                for (kc, bk) in ((0, bX), (1, bY)):
                    for sc in range(2):
                        B.cop("pe", lambda e, kc=kc, sc=sc, bk=bk: e.matmul(
                            bk.ap[:, 0:256],
                            lhsT=ka_.ap[:, sc, h * 256 + kc * 128:h * 256 + (kc + 1) * 128],
                            rhs=vw_.ap[:, sc * 4 + h, 0:256], start=(sc == 0), stop=(sc == 1)),
                            r=[ka_, vw_], w=[bk] if sc == 0 else (), pw=[bk] if sc else ())
                for vc in range(2):
                    B.cop("pe", lambda e, vc=vc: e.matmul(
                        bY.ap[:, 256:384], lhsT=ones_b, rhs=usq_.ap[:, vc, :], start=(vc == 0), stop=(vc == 1)),
                        r=[usq_, cb], pw=[bY])
                yield
                B.cop("act", lambda e: e.activation(out=t_.ap, in_=bY.ap[:, 256:384], func=AF.Ln,
                                                    scale=1.0 / 256, bias=EPS), r=[bY], w=[t_])
                B.cop("act", lambda e: e.activation(out=rs_.ap, in_=t_.ap, func=AF.Exp, scale=-0.5),
                      r=[t_], w=[rs_])
                for (kc, bk) in ((0, bX), (1, bY)):
                    B.cop("dve", lambda e, kc=kc, bk=bk: e.scalar_tensor_tensor(
                        out=C32.ap[:, kc, h, 0:256], in0=C32.ap[:, kc, h, 0:256], scalar=dec_.ap[:, h:h + 1],
                        in1=bk.ap[:, 0:256], op0=ALU.mult, op1=ALU.add),
                        r=[bk, dec_, C32], w=[C32])
                yield
                B.cop("act", lambda e: e.activation(out=cnext.ap[:, :, h, :], in_=C32.ap[:, :, h, :], func=AF.Copy),
                      r=[C32], w=[cnext] if h == 0 else (), pw=[cnext] if h else ())
                for vc in range(2):
                    B.cop("dve", lambda e, vc=vc: e.scalar_tensor_tensor(
                        out=g1_.ap[:, vc, :], in0=u_.ap[:, vc, :],
                        scalar=par[:, P_AHG + h * 2 + vc:P_AHG + h * 2 + vc + 1], in1=rs_.ap,
                        op0=ALU.mult, op1=ALU.mult), r=[u_, rs_, par], w=[g1_] if vc == 0 else (),
                        pw=[g1_] if vc else ())
                B.cop("pool", lambda e: e.tensor_tensor(
                    out=hast_.ap[:, h * 2:h * 2 + 2, :], in0=g1_.ap,
                    in1=za_.ap[:, h * 2:h * 2 + 2, :], op=ALU.mult), r=[g1_, za_],
                    w=[hast_] if h == 0 else (), pw=[hast_] if h else ())

            gens = [headpart(h) for h in range(4)]
            live = list(gens)
            while live:
                for g in list(live):
                    try:
                        next(g)
                    except StopIteration:
                        live.remove(g)
            B.dma("sp", HA.ap[:, :, i * 128:(i + 1) * 128].rearrange("c p t -> p c t"), hast_.ap, hast_,
                  r=[hast_], pw=[HA])
        sc_loads(0)
        for i in range(32):
            superchunk(i)
        B.barrier()
        B.release(e_mark)

    if stop_after is None:
        st32e = [B.alloc(f"st32e{i}", [128, 8, 512], F32) for i in range(2)]
        wa_, wb2_, wo_ = ewts
        haL = [B.alloc(f"haL{i}", [128, 8, 512], BF16) for i in range(2)]
        hbL = [B.alloc(f"hbL{i}", [128, 8, 512], BF16) for i in range(2)]
        gaL = [B.alloc(f"gaL{i}", [128, 8, 512], BF16) for i in range(2)]
        gbL = [B.alloc(f"gbL{i}", [128, 8, 512], BF16) for i in range(2)]
        xo = st32e
        mrg = B.alloc("mrg", [128, 8, 512], BF16)
        m1 = [B.alloc(f"m1{i}", [128, 512], F32) for i in range(2)]
        m2 = [B.alloc(f"m2{i}", [128, 512], F32) for i in range(2)]
        mc = 0
        for j in range(8):
            k = j % 2
            sl = slice(j * 512, (j + 1) * 512)
            for (L, Sx) in ((haL[k], HA), (hbL[k], HB), (gaL[k], GAT), (gbL[k], GBT)):
                B.dma("sp", L.ap, Sx.ap[:, :, sl].rearrange("c p t -> p c t"), L, r=[Sx], w=[L])
            xo_ = xo[k]
            B.dma("sp", xo_.ap, XT[2, :, sl].rearrange("(c p) t -> p c t", p=128), xo_, w=[xo_])
            for ec in range(8):
                pa, pb = nps(), nps()
                for c in range(8):
                    B.cop("pe", lambda e, pa=pa, c=c, ec=ec, k=k: e.matmul(
                        pa.ap, lhsT=wa_.ap[:, c, ec * 128:(ec + 1) * 128], rhs=haL[k].ap[:, c, :],
                        start=(c == 0), stop=(c == 7)), r=[wa_, haL[k]], w=[pa] if c == 0 else (), pw=[pa] if c else ())
                for c in range(8):
                    B.cop("pe", lambda e, pb=pb, c=c, ec=ec, k=k: e.matmul(
                        pb.ap, lhsT=wb2_.ap[:, c, ec * 128:(ec + 1) * 128], rhs=hbL[k].ap[:, c, :],
                        start=(c == 0), stop=(c == 7)), r=[wb2_, hbL[k]], w=[pb] if c == 0 else (), pw=[pb] if c else ())
                a1, a2 = m1[mc % 2], m2[mc % 2]
                mc += 1
                B.cop("dve", lambda e, pa=pa, a1=a1, ec=ec, k=k: e.tensor_tensor(out=a1.ap, in0=pa.ap, in1=gaL[k].ap[:, ec, :],
                                                                               op=ALU.mult), r=[pa, gaL[k]], w=[a1])
                B.cop("dve", lambda e, pb=pb, a2=a2, ec=ec, k=k: e.tensor_tensor(out=a2.ap, in0=pb.ap, in1=gbL[k].ap[:, ec, :],
                                                                               op=ALU.mult), r=[pb, gbL[k]], w=[a2])
                B.cop("pool", lambda e, a1=a1, a2=a2, ec=ec: e.tensor_tensor(out=mrg.ap[:, ec, :], in0=a1.ap, in1=a2.ap,
                                                                             op=ALU.add), r=[a1, a2],
                      w=[mrg] if ec == 0 else (), pw=[mrg] if ec else ())
            for ec in range(8):
                po = nps()
                for c in range(8):
                    B.cop("pe", lambda e, po=po, c=c, ec=ec: e.matmul(
                        po.ap, lhsT=wo_.ap[:, c, ec * 128:(ec + 1) * 128], rhs=mrg.ap[:, c, :],
                        start=(c == 0), stop=(c == 7)), r=[wo_, mrg], w=[po] if c == 0 else (), pw=[po] if c else ())
                B.cop("dve", lambda e, po=po, ec=ec, xo_=xo_: e.tensor_tensor(out=xo_.ap[:, ec, :], in0=po.ap,
                                                                             in1=xo_.ap[:, ec, :], op=ALU.add),
                      r=[po, xo_], pw=[xo_])
            B.dma("pool", OUT[:, sl].rearrange("(c p) t -> p c t", p=128), xo_.ap, xo_, r=[xo_])
    else:
        zt = B.alloc("zt", [128, 8, 512], F32)
        B.cop("dve", lambda e: e.memset(zt.ap, 0.0), w=[zt])
        for j in range(8):
            B.dma("pool", OUT[:, j * 512:(j + 1) * 512].rearrange("(c p) t -> p c t", p=128), zt.ap, zt, r=[zt])

    es = B.emit()
    return nc, B, es


def host_prep(core, x, norm_gain, w_in, conv_w, conv_b, i_bias, f_bias, a_head_gain, qk_gain_q, qk_gain_k,
              w_branch_a, w_branch_b, w_out):
    b, p = core // 2, core % 2
    xb = x[b]
    XTn = np.zeros((3, D, TOKW), np.float32)

    def halo(glist):
        h = np.zeros((len(glist) * 4, D), np.float32)
        for n, g in enumerate(glist):
            if g > 0:
                h[n * 4:(n + 1) * 4] = xb[128 * g - 4:128 * g]
        return h
    XTn[0, :, :OWN] = xb[0:OWN].T
    XTn[0, :, OWN:] = halo(list(range(0, 32))).T
    XTn[1, :, :OWN] = xb[OWN:].T
    XTn[1, :, OWN:] = halo(list(range(32, 64))).T
    own = [2 * i + p for i in range(32)]
    xo = xb.reshape(NBLK, 128, D)[own].reshape(OWN, D)
    XTn[2, :, :OWN] = xo.T
    XTn[2, :, OWN:] = halo(own).T

    par = np.zeros((128, NPAR), np.float32)
    par[:, P_GN:P_GN + 8] = norm_gain[0].reshape(8, 128).T
    cw = conv_w[0]
    par[:, P_CW:P_CW + 64] = cw.T.reshape(16, 128, 4).transpose(1, 0, 2).reshape(128, 64)
    par[:, P_CB:P_CB + 16] = conv_b[0].reshape(16, 128).T
    par[:, P_IB:P_IB + 4] = i_bias[0][None, :]
    par[:, P_FB:P_FB + 4] = f_bias[0][None, :]
    par[:, P_AHG:P_AHG + 8] = a_head_gain[0].reshape(8, 128).T
    par[:, P_GQ] = qk_gain_q[0]
    par[:, P_GK] = qk_gain_k[0]

    j = np.arange(128)[:, None]
    s = np.arange(128)[None, :]
    cf = np.zeros((128, NCF), np.float32)
    cf[:, CF_ONES:CF_ONES + 128] = 1.0
    cf[:, CF_TRI:CF_TRI + 128] = (j <= s)
    cf[:, CF_ID:CF_ID + 128] = np.eye(128)
    cbf = np.zeros((128, NCB), np.float32)
    cbf[:, CB_ONES:CB_ONES + 128] = 1.0
    cbf[:, CB_ID:CB_ID + 128] = np.eye(128)
    cbf[:, CB_TRI:CB_TRI + 128] = -1.0 * (j >= s)
    cbf[:, CB_REST:CB_REST + 128] = -1.0 * (j < s)
    ma = np.zeros((128, 8, 512), np.float32)
    for m in range(8):
        for a in range(4):
            kpos = m * 128 + np.arange(128)[:, None]
            qpos = (2 * a + p) * 128 + np.arange(128)[None, :]
            ma[:, m, a * 128:(a + 1) * 128] = (kpos < qpos)
    mm = np.zeros((128, 2, 128), np.float32)
    for jc in range(2):
        kpos = jc * 128 + np.arange(128)[:, None]
        qpos = p * 128 + np.arange(128)[None, :]
        mm[:, jc, :] = (kpos <= qpos)
    return {
        "xt": XTn, "w_in": np.ascontiguousarray(w_in[0]), "w_a": np.ascontiguousarray(w_branch_a[0]),
        "w_b": np.ascontiguousarray(w_branch_b[0]), "w_o": np.ascontiguousarray(w_out[0]),
        "params": par, "cf": cf, "cbf": cbf.astype(ml_dtypes.bfloat16), "ma": ma.astype(ml_dtypes.bfloat16),
        "mm": mm,
    }


_CACHE = {}


def kernel(**inputs):
    inputs = {k: np.asarray(v) for k, v in inputs.items()}
    if "nc" not in _CACHE:
        _CACHE["nc"] = build_program()
    nc, B, es = _CACHE["nc"]
    in_maps = [host_prep(c, **inputs) for c in range(8)]
    res = run_bass_kernel_spmd(nc, in_maps, core_ids=list(range(8)))
    out = np.zeros((4, S, D), np.float32)
    for c in range(8):
        b, p = c // 2, c % 2
        o = np.asarray(res.results[c]["out"]).T.reshape(32, 128, D)
        out[b].reshape(NBLK, 128, D)[p::2] = o
    return out
```
